# Optimizing a Trainium2 kernel written in Bass

```python
import math
import jax, jax.numpy as jnp
from jax import lax
import numpy as np

D_MODEL = 1024
BATCH = 4
SEQ = 8192
DEPTH = 1
DEC_BATCH = 16
DEC_SEQ = 64
PAST_LEN = 4096

CHUNK = 64
QBLOCK = 128
HEAD_DIM = 64
SB_HEADS = 8
DIFF_HEADS = 4
SB_WIDTH = SB_HEADS * HEAD_DIM
DIFF_WIDTH = DIFF_HEADS * 2 * HEAD_DIM
MIX_WIDTH = SB_WIDTH + DIFF_WIDTH
IN_WIDTH = 3 * MIX_WIDTH
IN_SPLITS = (SB_WIDTH, 2 * SB_WIDTH, 3 * SB_WIDTH, 3 * SB_WIDTH + DIFF_WIDTH, 3 * SB_WIDTH + 2 * DIFF_WIDTH)
D_FF = 2816
ROPE_THETA = 10000.0
LN_EPS = 1e-5
LAMBDA_STD = 0.1
DEEPNORM_ALPHA = (2 * DEPTH) ** 0.25
DEEPNORM_BETA = (8 * DEPTH) ** -0.25

kernel_name = "stickbreak_diffattn_macaron_streaming_step"


def lambda_init_for(layer):
    return 0.8 - 0.6 * math.exp(-0.3 * layer)


def layer_norm(x, g, b):
    xf = x.astype(jnp.float32)
    mu = jnp.mean(xf, axis=-1, keepdims=True)
    var = jnp.mean(jnp.square(xf - mu), axis=-1, keepdims=True)
    return ((xf - mu) * lax.rsqrt(var + LN_EPS)).astype(x.dtype) * g + b


def rms_norm(x, g):
    xf = x.astype(jnp.float32)
    return (xf * lax.rsqrt(jnp.mean(xf * xf, axis=-1, keepdims=True) + LN_EPS)).astype(x.dtype) * g


def rope(x, pos):
    half = HEAD_DIM // 2
    inv_freq = ROPE_THETA ** (-jnp.arange(half, dtype=jnp.float32) / half)
    ang = pos.astype(jnp.float32)[:, None] * inv_freq[None, :]
    bshape = (pos.shape[0],) + (1,) * (x.ndim - 3) + (half,)
    cos = jnp.cos(ang).reshape(bshape)
    sin = jnp.sin(ang).reshape(bshape)
    xf = x.astype(jnp.float32)
    x1, x2 = xf[..., :half], xf[..., half:]
    return jnp.concatenate([x1 * cos - x2 * sin, x2 * cos + x1 * sin], axis=-1).astype(x.dtype)


def swiglu(x, wg, wu, wd):
    gate = jnp.einsum('btd,df->btf', x, wg)
    up = jnp.einsum('btd,df->btf', x, wu)
    return jnp.einsum('btf,fd->btd', jax.nn.silu(gate) * up, wd)


def query_blocks(q):
    Tq = q.shape[1]
    qb = QBLOCK if Tq % QBLOCK == 0 else Tq
    nb = Tq // qb
    qs = q.reshape((q.shape[0], nb, qb) + q.shape[2:])
    return jnp.moveaxis(qs, 1, 0), qb, nb


def stick_breaking_attention(q, k, v, pos0):
    B, Tq, H, d = q.shape
    Tk = k.shape[1]
    scale = d ** -0.5
    k_pos = jnp.arange(Tk)
    qs, qb, nb = query_blocks(q)

    def block(args):
        qblk, i = args
        q_pos = pos0 + i * qb + jnp.arange(qb)
        z = jnp.einsum('bqhd,bkhd->bhqk', qblk, k, preferred_element_type=jnp.float32) * scale
        visible = k_pos[None, :] < q_pos[:, None]
        log_1m_beta = jnp.where(visible, jax.nn.log_sigmoid(-z), 0.0)
        stick = lax.cumsum(log_1m_beta, axis=3, reverse=True) - log_1m_beta
        w = jnp.where(visible, jnp.exp(jax.nn.log_sigmoid(z) + stick), 0.0)
        return jnp.einsum('bhqk,bkhd->bqhd', w.astype(v.dtype), v)

    out = lax.map(block, (qs, jnp.arange(nb)))
    return jnp.moveaxis(out, 0, 1).reshape(B, Tq, H, d)


def differential_attention(q, k, v, pos0, lam):
    B, Tq, H, _, d = q.shape
    Tk = k.shape[1]
    scale = d ** -0.5
    k_chunk = jnp.arange(Tk) // CHUNK
    qs, qb, nb = query_blocks(q)

    def block(args):
        qblk, i = args
        q_chunk = (pos0 + i * qb + jnp.arange(qb)) // CHUNK
        s = jnp.einsum('bqhcd,bkhcd->bhcqk', qblk, k, preferred_element_type=jnp.float32) * scale
        visible = k_chunk[None, :] <= q_chunk[:, None]
        p = jax.nn.softmax(jnp.where(visible, s, -jnp.inf), axis=-1)
        pd = p[:, :, 0] - lam * p[:, :, 1]
        return jnp.einsum('bhqk,bkhe->bqhe', pd.astype(v.dtype), v)

    out = lax.map(block, (qs, jnp.arange(nb)))
    return jnp.moveaxis(out, 0, 1).reshape(B, Tq, H, 2 * d)


def token_mixing(h, c_sb_k, c_sb_v, c_diff_k, c_diff_v, pos0, p, lambda_init):
    B, T, _ = h.shape
    proj = jnp.einsum('btd,de->bte', h, p['w_in'])
    sb_q, sb_k, sb_v, df_q, df_k, df_v = jnp.split(proj, IN_SPLITS, axis=-1)
    pos = pos0 + jnp.arange(T)
    sb_q = sb_q.reshape(B, T, SB_HEADS, HEAD_DIM)
    sb_k = sb_k.reshape(B, T, SB_HEADS, HEAD_DIM)
    sb_v = sb_v.reshape(B, T, SB_HEADS, HEAD_DIM)
    df_q = rope(df_q.reshape(B, T, DIFF_HEADS, 2, HEAD_DIM), pos)
    df_k = rope(df_k.reshape(B, T, DIFF_HEADS, 2, HEAD_DIM), pos)
    df_v = df_v.reshape(B, T, DIFF_HEADS, 2 * HEAD_DIM)

    if c_sb_k is None:
        sb_k_all, sb_v_all, df_k_all, df_v_all = sb_k, sb_v, df_k, df_v
    else:
        sb_k_all = jnp.concatenate([c_sb_k, sb_k], axis=1)
        sb_v_all = jnp.concatenate([c_sb_v, sb_v], axis=1)
        df_k_all = jnp.concatenate([c_diff_k, df_k], axis=1)
        df_v_all = jnp.concatenate([c_diff_v, df_v], axis=1)

    o_sb = stick_breaking_attention(sb_q, sb_k_all, sb_v_all, pos0).reshape(B, T, SB_WIDTH)

    lam = (jnp.exp(jnp.sum(p['lambda_q1'].astype(jnp.float32) * p['lambda_k1'].astype(jnp.float32)))
           - jnp.exp(jnp.sum(p['lambda_q2'].astype(jnp.float32) * p['lambda_k2'].astype(jnp.float32)))
           + lambda_init)
    o_df = differential_attention(df_q, df_k_all, df_v_all, pos0, lam)
    o_df = (rms_norm(o_df, p['subln_g']) * (1.0 - lambda_init)).reshape(B, T, DIFF_WIDTH)

    out = jnp.einsum('bte,ed->btd', jnp.concatenate([o_sb, o_df], axis=-1), p['w_o'])
    return out, (sb_k, sb_v, df_k, df_v)


def encoder_layer(x, c_sb_k, c_sb_v, c_diff_k, c_diff_v, pos0, p, lambda_init):
    a = DEEPNORM_ALPHA
    x = layer_norm(a * x + 0.5 * swiglu(x, p['ffn1_wg'], p['ffn1_wu'], p['ffn1_wd']), p['ln1_g'], p['ln1_b'])
    mix, rows = token_mixing(x, c_sb_k, c_sb_v, c_diff_k, c_diff_v, pos0, p, lambda_init)
    x = layer_norm(a * x + mix, p['ln2_g'], p['ln2_b'])
    x = layer_norm(a * x + 0.5 * swiglu(x, p['ffn2_wg'], p['ffn2_wu'], p['ffn2_wd']), p['ln3_g'], p['ln3_b'])
    return x, rows


def setup_inputs(seed: int = 0) -> dict:
    key = jax.random.key(seed)
    ks = jax.random.split(key, 40)
    L = DEPTH

    def nrm(k, shape, scale):
        return jax.random.normal(k, shape, jnp.float32) * scale

    fin = D_MODEL ** -0.5
    w_in = jnp.concatenate([
        nrm(ks[6], (L, D_MODEL, SB_WIDTH), fin),
        nrm(ks[7], (L, D_MODEL, SB_WIDTH), fin),
        nrm(ks[8], (L, D_MODEL, SB_WIDTH), fin * DEEPNORM_BETA),
        nrm(ks[9], (L, D_MODEL, DIFF_WIDTH), fin),
        nrm(ks[10], (L, D_MODEL, DIFF_WIDTH), fin),
        nrm(ks[11], (L, D_MODEL, DIFF_WIDTH), fin * DEEPNORM_BETA),
    ], axis=-1)
    return {
        "x_prompt": nrm(ks[0], (BATCH, SEQ, D_MODEL), 1.0),
        "x_sample": nrm(ks[1], (DEC_BATCH, DEC_SEQ, D_MODEL), 1.0),
        "cache_sb_k": nrm(ks[2], (L, DEC_BATCH, PAST_LEN, SB_HEADS, HEAD_DIM), 1.0),
        "cache_sb_v": nrm(ks[3], (L, DEC_BATCH, PAST_LEN, SB_HEADS, HEAD_DIM), 1.0),
        "cache_diff_k": nrm(ks[4], (L, DEC_BATCH, PAST_LEN, DIFF_HEADS, 2, HEAD_DIM), 1.0),
        "cache_diff_v": nrm(ks[5], (L, DEC_BATCH, PAST_LEN, DIFF_HEADS, 2 * HEAD_DIM), 1.0),
        "ln1_g": 1.0 + nrm(ks[12], (L, D_MODEL), 0.02),
        "ln1_b": nrm(ks[13], (L, D_MODEL), 0.02),
        "ffn1_wg": nrm(ks[14], (L, D_MODEL, D_FF), fin * DEEPNORM_BETA),
        "ffn1_wu": nrm(ks[15], (L, D_MODEL, D_FF), fin * DEEPNORM_BETA),
        "ffn1_wd": nrm(ks[16], (L, D_FF, D_MODEL), D_FF ** -0.5 * DEEPNORM_BETA),
        "w_in": w_in,
        "lambda_q1": nrm(ks[17], (L, HEAD_DIM), LAMBDA_STD),
        "lambda_k1": nrm(ks[18], (L, HEAD_DIM), LAMBDA_STD),
        "lambda_q2": nrm(ks[19], (L, HEAD_DIM), LAMBDA_STD),
        "lambda_k2": nrm(ks[20], (L, HEAD_DIM), LAMBDA_STD),
        "subln_g": 1.0 + nrm(ks[21], (L, 2 * HEAD_DIM), 0.02),
        "w_o": nrm(ks[22], (L, MIX_WIDTH, D_MODEL), MIX_WIDTH ** -0.5 * DEEPNORM_BETA),
        "ln2_g": 1.0 + nrm(ks[23], (L, D_MODEL), 0.02),
        "ln2_b": nrm(ks[24], (L, D_MODEL), 0.02),
        "ffn2_wg": nrm(ks[25], (L, D_MODEL, D_FF), fin * DEEPNORM_BETA),
        "ffn2_wu": nrm(ks[26], (L, D_MODEL, D_FF), fin * DEEPNORM_BETA),
        "ffn2_wd": nrm(ks[27], (L, D_FF, D_MODEL), D_FF ** -0.5 * DEEPNORM_BETA),
        "ln3_g": 1.0 + nrm(ks[28], (L, D_MODEL), 0.02),
        "ln3_b": nrm(ks[29], (L, D_MODEL), 0.02),
    }


def reference(x_prompt, x_sample, cache_sb_k, cache_sb_v, cache_diff_k, cache_diff_v,
              ln1_g, ln1_b, ffn1_wg, ffn1_wu, ffn1_wd, w_in,
              lambda_q1, lambda_k1, lambda_q2, lambda_k2, subln_g, w_o,
              ln2_g, ln2_b, ffn2_wg, ffn2_wu, ffn2_wd, ln3_g, ln3_b):
    yp, ys = x_prompt, x_sample
    rows_p = ([], [], [], [])
    rows_s = ([], [], [], [])
    for l in range(DEPTH):
        p = dict(ln1_g=ln1_g[l], ln1_b=ln1_b[l], ffn1_wg=ffn1_wg[l], ffn1_wu=ffn1_wu[l], ffn1_wd=ffn1_wd[l],
                 w_in=w_in[l], lambda_q1=lambda_q1[l], lambda_k1=lambda_k1[l],
                 lambda_q2=lambda_q2[l], lambda_k2=lambda_k2[l], subln_g=subln_g[l], w_o=w_o[l],
                 ln2_g=ln2_g[l], ln2_b=ln2_b[l], ffn2_wg=ffn2_wg[l], ffn2_wu=ffn2_wu[l], ffn2_wd=ffn2_wd[l],
                 ln3_g=ln3_g[l], ln3_b=ln3_b[l])
        lam0 = lambda_init_for(l)
        yp, rp = encoder_layer(yp, None, None, None, None, 0, p, lam0)
        ys, rs = encoder_layer(ys, cache_sb_k[l], cache_sb_v[l], cache_diff_k[l], cache_diff_v[l],
                               PAST_LEN, p, lam0)
        for acc, r in zip(rows_p, rp):
            acc.append(r)
        for acc, r in zip(rows_s, rs):
            acc.append(r)
    new_sb_k_p = jnp.stack(rows_p[0], axis=0)
    new_sb_v_p = jnp.stack(rows_p[1], axis=0)
    new_diff_k_p = jnp.stack(rows_p[2], axis=0)
    new_diff_v_p = jnp.stack(rows_p[3], axis=0)
    new_sb_k_s = jnp.stack(rows_s[0], axis=0)
    new_sb_v_s = jnp.stack(rows_s[1], axis=0)
    new_diff_k_s = jnp.stack(rows_s[2], axis=0)
    new_diff_v_s = jnp.stack(rows_s[3], axis=0)
    return (yp, ys, new_sb_k_p, new_sb_v_p, new_diff_k_p, new_diff_v_p,
            new_sb_k_s, new_sb_v_s, new_diff_k_s, new_diff_v_s)
```

```python
import math
import numpy as np
import ml_dtypes
import concourse.bass as bass
import concourse.mybir as mybir
from concourse.bass_utils import run_bass_kernel_spmd

F32 = mybir.dt.float32
BF16 = mybir.dt.bfloat16
AF = mybir.ActivationFunctionType
ALU = mybir.AluOpType
AX = mybir.AxisListType

D = 1024
DFF = 2816
NFC = 22
T = 8192
NG = 64
PAST = 4096
NKS = 33
TKS = 4224
ALPHA = 2.0 ** 0.25
EPS = 1e-5
LAM0 = 0.8 - 0.6 * math.exp(0.0)
NEGBIG = -30000.0
SAME_SYNC = True


class Buf:
    def __init__(self, t=None, name=""):
        self.t = t
        self.name = name
        self.lw = {}
        self.rd = {}
        self.dsem = None
        self.dn = 0


class Eng:
    def __init__(self, fw, name, is_pe=False):
        self.name = name
        self.is_pe = is_pe
        self.sem = fw.nc.alloc_semaphore("sem_" + name)
        self.n = 0
        self.waited = {}
        self.prog = []

    def wait(self, ev):
        sem, val = ev
        if sem is self.sem:
            if self.is_pe or not SAME_SYNC:
                return
            assert val <= self.n
        k = id(sem)
        if self.waited.get(k, 0) >= val:
            return
        self.prog.append(("wait", sem, val))
        self.waited[k] = val


class FW:
    def __init__(self, nc):
        self.nc = nc
        self.pe = Eng(self, "pe", True)
        self.act = Eng(self, "act")
        self.dve = Eng(self, "dve")
        self.pool = Eng(self, "pool")
        self.sp = Eng(self, "sp")
        self.out_events = []
        self.regs = {}
        self.ndsem = 0

    def sb(self, name, shape, dtype):
        return Buf(self.nc.alloc_sbuf_tensor(name, list(shape), dtype).ap(), name)

    def ps(self, name, shape, dtype=F32):
        return Buf(self.nc.alloc_psum_tensor(name, list(shape), dtype).ap(), name)

    def reg(self, key):
        b = self.regs.get(key)
        if b is None:
            b = Buf(None, str(key))
            self.regs[key] = b
        return b

    def _deps(self, E, reads, writes):
        for b in reads:
            for ev in b.lw.values():
                E.wait(ev)
        for b in writes:
            for ev in b.lw.values():
                E.wait(ev)
            for ev in b.rd.values():
                E.wait(ev)

    def _record(self, ev, reads, writes):
        k = id(ev[0])
        for b in writes:
            if b.t is None:
                b.lw[k] = ev
            else:
                b.lw = {k: ev}
                b.rd = {}
        for b in reads:
            if b in writes:
                continue
            old = b.rd.get(k)
            if old is None or old[1] < ev[1]:
                b.rd[k] = ev

    def op(self, E, reads, writes, meth, *args, signal=True, **kw):
        self._deps(E, reads, writes)
        if signal:
            E.prog.append(("op", meth, args, kw, E.sem))
            E.n += 1
            ev = (E.sem, E.n)
        else:
            E.prog.append(("op", meth, args, kw, None))
            ev = (E.sem, E.n + 1)
        self._record(ev, reads, writes)

    def dma(self, out_ap, in_ap, reads, writes, owner, is_output=False, chain=False, q=None):
        Q = self.sp if q is None else q
        if owner.dsem is None:
            owner.dsem = self.nc.alloc_semaphore("d%d" % self.ndsem)
            self.ndsem += 1
        self._deps(Q, reads, writes)
        if owner.dn > 0 and not chain:
            Q.wait((owner.dsem, 16 * owner.dn))
        Q.prog.append(("dma", out_ap, in_ap, owner.dsem))
        owner.dn += 1
        ev = (owner.dsem, 16 * owner.dn)
        self._record(ev, reads, writes)
        if is_output:
            self.out_events.append(ev)

    def finish(self):
        last = {}
        for ev in self.out_events:
            k = id(ev[0])
            if k not in last or last[k][1] < ev[1]:
                last[k] = ev
        for ev in last.values():
            self.sp.wait(ev)
        nc = self.nc
        with nc.Block() as block:
            for E, dec in ((self.sp, block.sync), (self.pe, block.tensor), (self.act, block.scalar),
                           (self.dve, block.vector), (self.pool, block.gpsimd)):
                def body(e, E=E):
                    for it in E.prog:
                        if it[0] == "wait":
                            e.wait_ge(it[1], it[2])
                        elif it[0] == "dma":
                            e.dma_start(out=it[1], in_=it[2]).then_inc(it[3], 16)
                        else:
                            ins = getattr(e, it[1])(*it[2], **it[3])
                            if it[4] is not None:
                                ins.then_inc(it[4], 1)
                dec(body)


def build():
    nc = bass.Bass("TRN2", target_bir_lowering=False)

    def din(name, shape, dt=F32):
        return nc.dram_tensor(name, list(shape), dt, kind="ExternalInput").ap()

    def dout(name, shape):
        return nc.dram_tensor(name, list(shape), F32, kind="ExternalOutput").ap()

    def dscr(name, shape, dt):
        return nc.dram_tensor(name, list(shape), dt, kind="Internal").ap()

    xp = din("xp", [T, D]); xs = din("xs", [128, D])
    c_k = [din("c_sbk", [2, PAST, 512]), din("c_dfk", [2, PAST, 512])]
    c_v = [din("c_sbv", [2, PAST, 512]), din("c_dfv", [2, PAST, 512])]
    f1 = (din("f1g", [D, DFF]), din("f1u", [D, DFF]), din("f1d", [DFF, D]))
    f2 = (din("f2g", [D, DFF]), din("f2u", [D, DFF]), din("f2d", [DFF, D]))
    win = din("win", [D, 3 * D]); wo = din("wo", [D, D])
    lnp = [din(n, [1, D]) for n in ("ln1g", "ln1b", "ln2g", "ln2b", "ln3g", "ln3b")]
    lamv = [din(n, [1, 64]) for n in ("lq1", "lk1", "lq2", "lk2")]
    subg = din("subg", [128, 1])
    cosp = din("cosp", [T, 32]); sinp = din("sinp", [T, 32])
    coss = din("coss", [128, 32]); sins = din("sins", [128, 32])
    kb0 = din("kb0", [128, 1])
    identd = din("ident", [128, 128], BF16); uincd = din("uinc", [128, 128], BF16)
    msbd = din("msb", [128, 4, 512], BF16); mdfd = din("mdf", [128, 4, 512], BF16)
    msmpd = din("msmp", [128, 64], BF16)

    yp = dout("yp", [T // 2, D]); ys = dout("ys", [128, D])
    o_kv = [dout(n, [T, 512]) for n in ("o_sbk", "o_sbv", "o_dfk", "o_dfv")]
    s_kv = [dout(n, [128, 512]) for n in ("s_sbk", "s_sbv", "s_dfk", "s_dfv")]

    X1 = dscr("X1", [T + 128, D], F32)
    X1T = dscr("X1T", [NG + 1, 128, D], BF16)
    QT = dscr("QT", [8, 128, T + 128], BF16)
    KT = dscr("KT", [8, 128, T], BF16)
    KTS = dscr("KTS", [2, 8, 128, TKS], BF16)
    VS = dscr("VS", [T, D], BF16)
    VSS = dscr("VSS", [2, TKS, D], BF16)
    OT = dscr("OT", [NG // 2 + 1, 128, 8, 128], BF16)
    X2 = dscr("X2", [T // 2 + 128, D], F32)

    fw = FW(nc)
    PE, ACT, DVE, POOL = fw.pe, fw.act, fw.dve, fw.pool
    sb = fw.sb
    Wg = sb("Wg", [128, 8, DFF], BF16); Wu = sb("Wu", [128, 8, DFF], BF16); Wd = sb("Wd", [128, NFC, D], BF16)
    F4 = [sb("F4_%d" % i, [128, D], F32) for i in range(6)]
    H2 = [sb("H2_%d" % i, [128, D], BF16) for i in range(8)]
    ST = [sb("ST_%d" % i, [128, 704], F32) for i in range(2)]
    HTB = sb("HTB", [128, DFF], BF16)
    MSK = F4[5]
    MSKv = F4[5].t.bitcast(BF16).rearrange("p (r n) -> p r n", n=512)
    MSMP = sb("MSMP", [128, 64], BF16)
    ident = sb("identb", [128, 128], BF16); uinc = sb("uincb", [128, 128], BF16); ones = sb("onesb", [128, 128], BF16)
    kb0t = sb("kb0t", [128, 1], F32); zcol = sb("zcol", [128, 1], F32); epsc = sb("epsc", [128, 1], F32)
    gcol = sb("gcol", [128, 1], F32); nlam = sb("nlam", [128, 1], F32)
    lamt = sb("lamt", [128, 4, 64], F32); lamp = sb("lamp", [128, 2, 64], F32); lams = sb("lams", [128, 2], F32)
    stats = sb("stats", [128, 2, 6], F32); mv = sb("mv", [128, 2], F32); lnv = sb("lnv", [128, 1], F32); rstd = sb("rstd", [128, 1], F32)
    cst = sb("cst", [128, 2, 32], F32)
    RT = [sb("RT_%d" % i, [128, 8, 32], F32) for i in range(2)]
    PSALL = nc.alloc_psum_tensor("PSALL", [128, 4096], F32).ap()
    PS = [Buf(PSALL[:, i * 512:(i + 1) * 512], "PS%d" % i) for i in range(8)]
    ones32 = sb("ones32", [128, 128], F32)

    def psbf(b):
        return b.t.bitcast(BF16)

    def pbc(ap1):
        return ap1.partition_broadcast(128)

    fw.dma(ident.t, identd, [], [ident], ident)
    fw.dma(uinc.t, uincd, [], [uinc], uinc)
    fw.dma(kb0t.t, kb0, [], [kb0t], kb0t)
    fw.dma(gcol.t, subg, [], [gcol], gcol)
    fw.dma(MSMP.t, msmpd, [], [MSMP], MSMP)
    for i in range(4):
        fw.dma(lamt.t[:, i:i + 1, :], pbc(lamv[i]), [], [lamt], lamt)
    fw.op(POOL, [], [ones], "memset", ones.t, 1.0)
    fw.op(POOL, [], [ones32], "memset", ones32.t, 1.0)
    fw.op(POOL, [], [zcol], "memset", zcol.t, 0.0)
    fw.op(POOL, [], [epsc], "memset", epsc.t, EPS)
    epsl = sb("epsl", [128, 1], F32)
    fw.op(POOL, [], [epsl], "memset", epsl.t, EPS / (ALPHA * ALPHA))
    fw.op(DVE, [gcol], [gcol], "tensor_scalar", gcol.t, gcol.t, 1.0 - LAM0, None, op0=ALU.mult)
    fw.op(DVE, [lamt], [lamp], "tensor_tensor", lamp.t[:, 0, :], lamt.t[:, 0, :], lamt.t[:, 1, :], op=ALU.mult)
    fw.op(DVE, [lamt], [lamp], "tensor_tensor", lamp.t[:, 1, :], lamt.t[:, 2, :], lamt.t[:, 3, :], op=ALU.mult)
    fw.op(DVE, [lamp], [lams], "reduce_sum", lams.t, lamp.t, axis=AX.X)
    fw.op(ACT, [lams], [lams], "activation", lams.t, lams.t, AF.Exp)
    fw.op(DVE, [lams], [nlam], "tensor_tensor", nlam.t, lams.t[:, 1:2], lams.t[:, 0:1], op=ALU.subtract)
    fw.op(DVE, [nlam], [nlam], "tensor_scalar", nlam.t, nlam.t, -LAM0, None, op0=ALU.add)

    cnt = {"st": 0}

    def load_cast(dstb, dst_ap, src_ap, n):
        k = cnt["st"]; cnt["st"] += 1
        st = ST[k % 2]
        fw.dma(st.t[:, 0:n], src_ap, [], [st], st)
        E = DVE if k % 2 == 0 else POOL
        fw.op(E, [st], [dstb], "tensor_copy", dst_ap, st.t[:, 0:n])

    def load_ffn_w(ws):
        for dc in range(8):
            for q in range(4):
                load_cast(Wg, Wg.t[:, dc, q * 704:(q + 1) * 704], ws[0][dc * 128:(dc + 1) * 128, q * 704:(q + 1) * 704], 704)
        for dc in range(8):
            for q in range(4):
                load_cast(Wu, Wu.t[:, dc, q * 704:(q + 1) * 704], ws[1][dc * 128:(dc + 1) * 128, q * 704:(q + 1) * 704], 704)
        for fc in range(NFC):
            for q in range(2):
                load_cast(Wd, Wd.t[:, fc, q * 512:(q + 1) * 512], ws[2][fc * 128:(fc + 1) * 128, q * 512:(q + 1) * 512], 512)

    def ln_stats(rt):
        for h in range(2):
            fw.op(DVE, [rt], [stats], "bn_stats", stats.t[:, h, :], rt.t[:, h * 512:(h + 1) * 512])
        fw.op(DVE, [stats], [mv], "bn_aggr", mv.t, stats.t)

    def ln_act():
        fw.op(ACT, [mv, epsl], [lnv], "activation", lnv.t, mv.t[:, 1:2], AF.Ln, bias=epsl.t, scale=1.0)
        fw.op(ACT, [lnv], [rstd], "activation", rstd.t, lnv.t, AF.Exp, scale=-0.5)

    def ln_apply(rt):
        fw.op(DVE, [rt, mv, rstd], [rt], "tensor_scalar", rt.t, rt.t, mv.t[:, 0:1], rstd.t, op0=ALU.subtract, op1=ALU.mult)
        fw.op(POOL, [rt, F4[4]], [rt], "tensor_tensor", rt.t, rt.t, F4[4].t, op=ALU.mult)
        fw.op(POOL, [rt, F4[5]], [rt], "tensor_tensor", rt.t, rt.t, F4[5].t, op=ALU.add)

    def layernorm(rt):
        ln_stats(rt)
        ln_act()
        ln_apply(rt)

    def load_ln(gi):
        fw.dma(F4[4].t.unsqueeze(1), pbc(lnp[gi]), [], [F4[4]], F4[4])
        fw.dma(F4[5].t.unsqueeze(1), pbc(lnp[gi + 1]), [], [F4[5]], F4[5])

    def transpose8(src, psb, dst, dst_ap=None):
        pb = psbf(psb)
        for c in range(8):
            fw.op(PE, [src, ident], [psb], "transpose", pb[:, c * 128:(c + 1) * 128], src.t[:, c * 128:(c + 1) * 128], ident.t, signal=(c == 7))
        fw.op(ACT, [psb], [dst], "copy", dst.t if dst_ap is None else dst_ap, pb)

    def ffn_stage(ngroups, src_fn, ws, lni, dst_fn, x1t):
        load_ffn_w(ws)
        load_ln(lni)
        nfb = (DFF + 511) // 512
        HB = []
        for fb_ in range(nfb):
            par = H2[5 + fb_ // 2]
            hbuf = Buf(par.t[:, (fb_ % 2) * 512:(fb_ % 2) * 512 + min(512, DFF - fb_ * 512)], "HB%d" % fb_)
            hbuf.lw = dict(par.lw); hbuf.rd = dict(par.rd)
            HB.append(hbuf)

        def ld(g):
            xt = F4[g % 2]
            ap, rb = src_fn(g)
            fw.dma(xt.t, ap, rb, [xt], xt)

        def PRE_cast(g):
            xt = F4[g % 2]
            xbf = H2[0]
            fw.op(ACT, [xt], [xbf], "copy", xbf.t, xt.t)
            if g + 1 < ngroups:
                ld(g + 1)

        def PRE_tr(g):
            transpose8(H2[0], PS[7], H2[1 + g % 2])

        def PRE(g):
            PRE_cast(g)
            PRE_tr(g)

        def ldr(g):
            rt = F4[2 + g % 2]
            ap, rb = src_fn(g)
            fw.dma(rt.t, ap, rb, [rt], rt)

        def GU(g, fb):
            xT = H2[1 + g % 2]
            f0 = fb * 512
            fn = min(512, DFF - f0)
            pg = PS[fb % 2]; pu = PS[2 + fb % 2]
            for dc in range(8):
                fw.op(PE, [Wg, xT], [pg], "matmul", pg.t[:, 0:fn], xT.t[:, dc * 128:(dc + 1) * 128], Wg.t[:, dc, f0:f0 + fn], start=(dc == 0), stop=(dc == 7), signal=(dc == 7))
            for dc in range(8):
                fw.op(PE, [Wu, xT], [pu], "matmul", pu.t[:, 0:fn], xT.t[:, dc * 128:(dc + 1) * 128], Wu.t[:, dc, f0:f0 + fn], start=(dc == 0), stop=(dc == 7), signal=(dc == 7))
            sg = ST[fb % 2]
            hb = HB[fb]
            fw.op(ACT, [pg], [sg], "activation", sg.t[:, 0:fn], pg.t[:, 0:fn], AF.Silu)
            fw.op(DVE, [sg, pu], [hb], "tensor_tensor", hb.t[:, 0:fn], sg.t[:, 0:fn], pu.t[:, 0:fn], op=ALU.mult)

        def TR(g, fb):
            f0 = fb * 512
            fn = min(512, DFF - f0)
            hb = HB[fb]
            hc = 0
            pb = PS[6 + fb % 2]
            pbv = psbf(pb)
            nch = fn // 128
            for c in range(nch):
                fw.op(PE, [hb, ident], [pb], "transpose", pbv[:, c * 128:(c + 1) * 128], hb.t[:, hc + c * 128:hc + (c + 1) * 128], ident.t, signal=(c == nch - 1))
            fw.op(ACT, [pb], [HTB], "copy", HTB.t[:, f0:f0 + fn], pbv[:, 0:fn])

        def DOWN(g):
            for half in range(2):
                for fc in range(NFC):
                    fw.op(PE, [HTB, Wd], [PS[4 + half]], "matmul", PS[4 + half].t, HTB.t[:, fc * 128:(fc + 1) * 128], Wd.t[:, fc, half * 512:(half + 1) * 512], start=(fc == 0), stop=(fc == NFC - 1), signal=(fc == NFC - 1))

        def POST_a(g):
            rt = F4[2 + g % 2]
            for half in range(2):
                fw.op(DVE, [PS[4 + half], rt], [rt], "scalar_tensor_tensor", rt.t[:, half * 512:(half + 1) * 512], PS[4 + half].t, 0.5 / ALPHA, rt.t[:, half * 512:(half + 1) * 512], op0=ALU.mult, op1=ALU.add)
            ln_stats(rt)

        def POST_b(g):
            rt = F4[2 + g % 2]
            ln_apply(rt)
            dst_fn(g, rt)
            if x1t:
                x1bf = H2[3]
                fw.op(POOL, [rt], [x1bf], "tensor_copy", x1bf.t, rt.t)

        def X1TR(g):
            x1bf = H2[3]
            x1T = H2[4]
            transpose8(x1bf, PS[7], x1T)
            r = fw.reg(("X1T", g))
            fw.dma(X1T[g], x1T.t, [x1T], [r], x1T)

        ld(0)
        PRE(0)
        for g in range(ngroups):
            for fb in range(nfb):
                GU(g, fb)
                if fb >= 1:
                    TR(g, fb - 1)
                if fb == 0:
                    if g > 0:
                        TR(g - 1, nfb - 1)
                        DOWN(g - 1)
                    ldr(g)
                if fb == 1:
                    if g > 0:
                        POST_a(g - 1)
                    if g + 1 < ngroups:
                        PRE_cast(g + 1)
                    if g > 1 and x1t:
                        X1TR(g - 2)
                if fb == 2:
                    if g + 1 < ngroups:
                        PRE_tr(g + 1)
                    if g > 0:
                        ln_act()
                if fb == 3 and g > 0:
                    POST_b(g - 1)
        TR(ngroups - 1, nfb - 1)
        DOWN(ngroups - 1)
        POST_a(ngroups - 1)
        ln_act()
        if x1t and ngroups > 1:
            X1TR(ngroups - 2)
        POST_b(ngroups - 1)
        for fb_ in range(nfb):
            par = H2[5 + fb_ // 2]
            for d_src, d_dst in ((HB[fb_].lw, par.lw), (HB[fb_].rd, par.rd)):
                for k, ev in d_src.items():
                    if k not in d_dst or d_dst[k][1] < ev[1]:
                        d_dst[k] = ev
        if x1t:
            X1TR(ngroups - 1)

    def srcA(g):
        if g < NG:
            return xp[g * 128:(g + 1) * 128, :], []
        return xs, []

    def dstA(g, rt):
        r = fw.reg(("X1", g))
        fw.dma(X1[g * 128:(g + 1) * 128, :], rt.t, [rt], [r], rt)

    ffn_stage(NG + 1, srcA, f1, 0, dstA, True)

    for dc in range(8):
        for cb in range(6):
            dstb = Wg if cb < 4 else Wu
            c0 = (cb % 4) * 512
            load_cast(dstb, dstb.t[:, dc, c0:c0 + 512], win[dc * 128:(dc + 1) * 128, cb * 512:(cb + 1) * 512], 512)

    def wview(dc, cb):
        b = Wg if cb < 4 else Wu
        c0 = (cb % 4) * 512
        return b, b.t[:, dc, c0:c0 + 512]

    def rope(psb, outb, out_ap):
        x = psb.t.rearrange("p (a b) -> p a b", b=64)
        o = out_ap.rearrange("p (a b) -> p a b", b=64)
        cb_ = cst.t[:, 0:1, :].to_broadcast([128, 8, 32]); sb_ = cst.t[:, 1:2, :].to_broadcast([128, 8, 32])
        tA, tB = RT
        fw.op(DVE, [psb, cst], [tA], "tensor_tensor", tA.t, x[:, :, 0:32], cb_, op=ALU.mult)
        fw.op(DVE, [psb, cst], [tB], "tensor_tensor", tB.t, x[:, :, 32:64], sb_, op=ALU.mult)
        fw.op(DVE, [tA, tB], [outb], "tensor_tensor", o[:, :, 0:32], tA.t, tB.t, op=ALU.subtract)
        fw.op(DVE, [psb, cst], [tA], "tensor_tensor", tA.t, x[:, :, 32:64], cb_, op=ALU.mult)
        fw.op(DVE, [psb, cst], [tB], "tensor_tensor", tB.t, x[:, :, 0:32], sb_, op=ALU.mult)
        fw.op(DVE, [tA, tB], [outb], "tensor_tensor", o[:, :, 32:64], tA.t, tB.t, op=ALU.add)

    bcnt = {"n": 0}

    def ldB(g):
        xt = H2[g % 2]
        fw.dma(xt.t, X1T[g], [fw.reg(("X1T", g))], [xt], xt)
    ldB(0)
    for g in range(NG + 1):
        if g + 1 < NG + 1:
            ldB(g + 1)
        x1T = H2[g % 2]
        smp = g == NG
        if smp:
            fw.dma(cst.t[:, 0, :], coss, [], [cst], cst); fw.dma(cst.t[:, 1, :], sins, [], [cst], cst)
        else:
            fw.dma(cst.t[:, 0, :], cosp[g * 128:(g + 1) * 128, :], [], [cst], cst)
            fw.dma(cst.t[:, 1, :], sinp[g * 128:(g + 1) * 128, :], [], [cst], cst)
        qb, kb, vb = H2[2], H2[3], H2[4]
        qT, kT = H2[5], H2[6]

        def blk(cb):
            k_ = bcnt["n"]; bcnt["n"] += 1
            pb_ = PS[k_ % 4]
            for dc in range(8):
                wb, wap = wview(dc, cb)
                fw.op(PE, [x1T, wb], [pb_], "matmul", pb_.t, x1T.t[:, dc * 128:(dc + 1) * 128], wap, start=(dc == 0), stop=(dc == 7), signal=(dc == 7))
            fo = F4[k_ % 4]
            foa = fo.t[:, 0:512]
            if cb == 0:
                fw.op(ACT, [pb_], [qb], "activation", qb.t[:, 0:512], pb_.t, AF.Identity, scale=0.125)
            elif cb == 3:
                rope(pb_, fo, foa)
                fw.op(ACT, [fo], [qb], "activation", qb.t[:, 512:1024], foa, AF.Identity, scale=0.125)
            else:
                if cb == 4:
                    rope(pb_, fo, foa)
                else:
                    fw.op(ACT, [pb_], [fo], "copy", foa, pb_.t)
                oi = {1: 0, 2: 1, 4: 2, 5: 3}[cb]
                if smp:
                    fw.dma(s_kv[oi], foa, [fo], [], fo, is_output=True)
                else:
                    fw.dma(o_kv[oi][g * 128:(g + 1) * 128, :], foa, [fo], [], fo, is_output=True)
                tb = kb if cb in (1, 4) else vb
                c0 = 0 if cb in (1, 2) else 512
                if cb in (1, 4):
                    fw.op(ACT, [fo], [tb], "copy", tb.t[:, c0:c0 + 512], foa)
                else:
                    fw.op(POOL, [fo], [tb], "tensor_copy", tb.t[:, c0:c0 + 512], foa)

        def trq(g_):
            transpose8(qb, PS[6], qT)
            fw.dma(QT.rearrange("c p t -> p c t")[:, :, g_ * 128:(g_ + 1) * 128], qT.t.rearrange("p (c t) -> p c t", t=128), [qT], [fw.reg("QT")], qT)

        def trk(g_):
            transpose8(kb, PS[7], kT)
            if g_ == NG:
                for be in range(2):
                    fw.dma(KTS[be].rearrange("c p t -> p c t")[:, :, PAST:PAST + 64], kT.t.rearrange("p (c t) -> p c t", t=128)[:, :, be * 64:(be + 1) * 64], [kT], [fw.reg(("KTS", be))], kT)
            else:
                fw.dma(KT.rearrange("c p t -> p c t")[:, :, g_ * 128:(g_ + 1) * 128], kT.t.rearrange("p (c t) -> p c t", t=128), [kT], [fw.reg("KT")], kT)

        blk(0)
        if g > 0:
            trk(g - 1)
        blk(3)
        blk(1)
        blk(4)
        trq(g)
        blk(2)
        blk(5)
        if smp:
            for be in range(2):
                fw.dma(VSS[be, PAST:PAST + 64, :], vb.t[be * 64:(be + 1) * 64, :], [vb], [fw.reg(("VSS", be))], vb)
        else:
            fw.dma(VS[g * 128:(g + 1) * 128, :], vb.t, [vb], [fw.reg("VS")], vb)
        if g == NG:
            trk(g)

    b2 = [(be, kbk) for be in range(2) for kbk in range(PAST // 128)]

    def ldB2(it):
        be, kbk = b2[it]
        rows = slice(kbk * 128, (kbk + 1) * 128)
        ck = F4[it % 2]; cv = F4[2 + it % 2]
        for i in range(2):
            fw.dma(ck.t[:, i * 512:(i + 1) * 512], c_k[i][be, rows, :], [], [ck], ck)
            fw.dma(cv.t[:, i * 512:(i + 1) * 512], c_v[i][be, rows, :], [], [cv], cv)
    ldB2(0)
    for it in range(len(b2)):
        if it + 1 < len(b2):
            ldB2(it + 1)
        be, kbk = b2[it]
        rows = slice(kbk * 128, (kbk + 1) * 128)
        ck = F4[it % 2]; cv = F4[2 + it % 2]
        kbf = H2[it % 2]; vbf = H2[2 + it % 2]
        fw.op(DVE, [ck], [kbf], "tensor_copy", kbf.t, ck.t)
        fw.op(POOL, [cv], [vbf], "tensor_copy", vbf.t, cv.t)
        fw.dma(VSS[be, rows, :], vbf.t, [vbf], [fw.reg(("VSS", be))], vbf)
        kT = H2[4 + it % 2]
        transpose8(kbf, PS[it % 2], kT)
        fw.dma(KTS[be].rearrange("c p t -> p c t")[:, :, rows], kT.t.rearrange("p (c t) -> p c t", t=128), [kT], [fw.reg(("KTS", be))], kT)

    WdF = Wd.t.rearrange("p a b -> p (a b)")
    WgF = Wg.t.rearrange("p a b -> p (a b)")
    WuF = Wu.t.rearrange("p a b -> p (a b)")
    cur = {}

    def load_kv(Kb, Ksb, Vb, Vsb_, kt_src, kt_reg, v_src, v_reg, nk, vcol):
        n = nk * 128
        for c0 in range(0, n, 2048):
            c1 = min(n, c0 + 2048)
            fw.dma(Ksb[:, c0:c1], kt_src[:, c0:c1], [kt_reg], [Kb], Kb, chain=True, q=POOL)
        vv = v_src.rearrange("(n p) c -> p n c", p=128)
        vs = Vsb_[:, 0:nk * 128].rearrange("p (n c) -> p n c", c=128)
        for k0 in range(0, nk, 8):
            k1 = min(nk, k0 + 8)
            fw.dma(vs[:, k0:k1, :], vv[:, k0:k1, vcol:vcol + 128], [v_reg], [Vb], Vb, chain=True, q=POOL)

    def kvset(kind, be=0):
        if kind == "p":
            return Wg, WgF[:, 0:T], Wu, WuF[:, 0:T]
        return Wd, WdF[:, be * TKS:(be + 1) * TKS], Wd, WdF[:, (2 + be) * TKS:(3 + be) * TKS]

    def ps2(b0):
        return PSALL[:, b0 * 512:(b0 + 2) * 512].rearrange("p (h n) -> p h n", h=2)

    def v2(buf):
        return buf.t.rearrange("p (h n) -> p h n", h=2)

    def load_q(qb, qz_ap, qsrc, qreg, N):
        fw.dma(qz_ap[0:64, 0:N], qsrc[0:64, :], [qreg], [qb], qb)
        fw.dma(qz_ap[64:128, 512:512 + N], qsrc[64:128, :], [qreg], [qb], qb, chain=True)

    QB = []
    for k in range(2):
        v = F4[2 + k].t.bitcast(BF16)
        QB.append((F4[2 + k], v[:, 0:1024], v[:, 1024:2048]))

    def load_q_sb(k, qsrc, qreg, N):
        qb, qz_ap, nq_ap = QB[k]
        load_q(qb, qz_ap, qsrc, qreg, N)
        fw.op(DVE, [qb], [qb], "tensor_scalar", nq_ap, qz_ap, -1.0, None, op0=ALU.mult)

    def sb_slot(qk, N, units, odst, oreg):
        Wg, KTsb, Wu, Vsb = cur["kv"]
        qz, qz_ap, nq_ap = QB[qk]
        nq = qz
        R = H2[4]
        ob = H2[7]
        nu = len(units)
        accs = (PS[6], PS[7])
        tpb = (PS[4], PS[5]); tp2 = ps2(4)
        fw.op(DVE, [], [R], "memset", R.t, 0.0)

        def bufs(ui):
            p = ui % 2
            return (PS[2 * p], PS[2 * p + 1]), ps2(2 * p), F4[p], H2[p], H2[2 + p]

        def stA(ui):
            kb, kn, mbuf, map_, bias = units[ui]
            zb_, z2, e, sp, w = bufs(ui)
            for hh in range(2):
                fw.op(PE, [Wg, qz], [zb_[hh]], "matmul", z2[0:kn, hh, 0:N], KTsb[:, kb * 128:kb * 128 + kn], qz_ap[:, hh * 512:hh * 512 + N], start=True, stop=True)

        def stB(ui):
            kb, kn, mbuf, map_, bias = units[ui]
            zb_, z2, e, sp, w = bufs(ui)
            bb, bap = bias
            fw.op(ACT, [zb_[0], zb_[1], bb], [e], "activation", v2(e)[0:kn, :, 0:N], z2[0:kn, :, 0:N], AF.Exp, bias=bap[0:kn, :], scale=1.0)

        def stB2(ui):
            kb, kn, mbuf, map_, bias = units[ui]
            zb_, z2, e, sp, w = bufs(ui)
            fw.op(ACT, [e], [sp], "activation", v2(sp)[0:kn, :, 0:N], v2(e)[0:kn, :, 0:N], AF.Ln, bias=1.0, scale=1.0)
            if mbuf is not None:
                for hh in range(2):
                    fw.op(DVE, [sp, mbuf], [sp], "tensor_tensor", sp.t[0:kn, hh * 512:hh * 512 + N], sp.t[0:kn, hh * 512:hh * 512 + N], map_, op=ALU.mult)

        def stC(ui):
            kb, kn, mbuf, map_, bias = units[ui]
            zb_, z2, e, sp, w = bufs(ui)
            first = ui == 0
            ks = slice(kb * 128, kb * 128 + kn)
            for hh in range(2):
                cs = slice(hh * 512, hh * 512 + N)
                fw.op(PE, [Wg, nq], [tpb[hh]], "matmul", tp2[0:kn, hh, 0:N], KTsb[:, ks], nq_ap[:, cs], start=True, stop=False, signal=False)
                fw.op(PE, [uinc, sp], [tpb[hh]], "matmul", tp2[0:kn, hh, 0:N], uinc.t[0:kn, 0:kn], sp.t[0:kn, cs], start=False, stop=first, signal=first)
                if not first:
                    fw.op(PE, [ones, R], [tpb[hh]], "matmul", tp2[0:kn, hh, 0:N], ones.t[:, 0:kn], R.t[:, cs], start=False, stop=True)
            if ui != nu - 1:
                fw.op(DVE, [R, sp], [R], "tensor_tensor", v2(R)[0:kn, :, 0:N], v2(R)[0:kn, :, 0:N], v2(sp)[0:kn, :, 0:N], op=ALU.add)

        def stD(ui):
            kb, kn, mbuf, map_, bias = units[ui]
            zb_, z2, e, sp, w = bufs(ui)
            bb, bap = bias
            fw.op(ACT, [tpb[0], tpb[1], bb], [w], "activation", v2(w)[0:kn, :, 0:N], tp2[0:kn, :, 0:N], AF.Exp, bias=bap[0:kn, :], scale=-1.0)
            if mbuf is not None:
                for hh in range(2):
                    fw.op(DVE, [w, mbuf], [w], "tensor_tensor", w.t[0:kn, hh * 512:hh * 512 + N], w.t[0:kn, hh * 512:hh * 512 + N], map_, op=ALU.mult)

        def stE(ui):
            kb, kn, mbuf, map_, bias = units[ui]
            zb_, z2, e, sp, w = bufs(ui)
            for hh in range(2):
                fw.op(PE, [Wu, w], [accs[hh]], "matmul", accs[hh].t[:, 0:N], Vsb[0:kn, kb * 128:(kb + 1) * 128], w.t[0:kn, hh * 512:hh * 512 + N], start=(ui == 0), stop=(ui == nu - 1), signal=(ui == nu - 1))

        for s_ in range(-2, nu):
            if 0 <= s_ + 2 < nu:
                stA(s_ + 2)
            if 0 <= s_ + 1 < nu:
                stB(s_ + 1)
                stB2(s_ + 1)
            if 0 <= s_ < nu:
                stD(s_)
            if 0 <= s_ + 1 < nu:
                stC(s_ + 1)
            if 0 <= s_ < nu:
                stE(s_)
        for hh in range(2):
            pr = slice(64 * hh, 64 * hh + 64)
            fw.op(DVE, [accs[hh]], [ob], "tensor_copy", ob.t[pr, 0:N], accs[hh].t[pr, 0:N])
        fw.dma(odst, ob.t[:, 0:N].rearrange("p (g t) -> p g t", t=128) if N == 512 else ob.t[:, 0:N], [ob], [oreg], ob)

    def df_slot(qk, N, units, odst, oreg):
        Wg, KTsb, Wu, Vsb = cur["kv"]
        qz = H2[5 + qk]
        ob = H2[7]
        accs = (PS[4], PS[5])
        psacc = F4[4]
        nu = len(units)
        fw.op(DVE, [], [psacc], "memset", psacc.t, 0.0)

        def bufs(ui):
            p = ui % 3
            b0 = (0, 2, 6)[p]
            return (PS[b0], PS[b0 + 1]), ps2(b0), H2[p]

        def stA(ui):
            kb, kn, mbuf, map_, bias = units[ui]
            sb_, s2, p = bufs(ui)
            for m in range(2):
                fw.op(PE, [Wg, qz], [sb_[m]], "matmul", s2[0:kn, m, 0:N], KTsb[:, kb * 128:kb * 128 + kn], qz.t[:, m * 512:m * 512 + N], start=True, stop=True)

        def stB(ui):
            kb, kn, mbuf, map_, bias = units[ui]
            sb_, s2, p = bufs(ui)
            bb, bap = bias
            fw.op(ACT, [sb_[0], sb_[1], bb], [p], "activation", v2(p)[0:kn, :, 0:N], s2[0:kn, :, 0:N], AF.Exp, bias=bap[0:kn, :], scale=1.0)
            if mbuf is not None:
                for m in range(2):
                    fw.op(DVE, [p, mbuf], [p], "tensor_tensor", p.t[0:kn, m * 512:m * 512 + N], p.t[0:kn, m * 512:m * 512 + N], map_, op=ALU.mult)
            fw.op(DVE, [psacc, p], [psacc], "tensor_tensor", v2(psacc)[0:kn, :, 0:N], v2(psacc)[0:kn, :, 0:N], v2(p)[0:kn, :, 0:N], op=ALU.add)

        def stC(ui):
            kb, kn, mbuf, map_, bias = units[ui]
            sb_, s2, p = bufs(ui)
            first = ui == 0; last = ui == nu - 1
            for m in range(2):
                fw.op(PE, [Wu, p], [accs[m]], "matmul", accs[m].t[:, 0:N], Vsb[0:kn, kb * 128:(kb + 1) * 128], p.t[0:kn, m * 512:m * 512 + N], start=first, stop=last, signal=last)

        for s_ in range(-2, nu):
            if 0 <= s_ + 2 < nu:
                stA(s_ + 2)
            if 0 <= s_ < nu:
                stB(s_)
                stC(s_)
        sums = (PS[6], PS[7])
        for m in range(2):
            fw.op(PE, [ones32, psacc], [sums[m]], "matmul", sums[m].t[:, 0:N], ones32.t, psacc.t[:, m * 512:m * 512 + N], start=True, stop=True)
        r0 = F4[0]; r1 = F4[1]; a0 = F4[2]; o = F4[3]
        for sm_, r_ in ((sums[0], r0), (sums[1], r1)):
            fw.op(ACT, [sm_], [r_], "activation", r_.t[:, 0:N], sm_.t[:, 0:N], AF.Ln)
            fw.op(ACT, [r_], [r_], "activation", r_.t[:, 0:N], r_.t[:, 0:N], AF.Exp, scale=-1.0)
        fw.op(DVE, [accs[0], r0], [a0], "tensor_tensor", a0.t[:, 0:N], accs[0].t[:, 0:N], r0.t[:, 0:N], op=ALU.mult)
        fw.op(DVE, [accs[1], r1], [r1], "tensor_tensor", r1.t[:, 0:N], accs[1].t[:, 0:N], r1.t[:, 0:N], op=ALU.mult)
        fw.op(DVE, [r1, nlam, a0], [o], "scalar_tensor_tensor", o.t[:, 0:N], r1.t[:, 0:N], nlam.t, a0.t[:, 0:N], op0=ALU.mult, op1=ALU.add)
        sq = H2[3]
        fw.op(DVE, [o], [sq], "tensor_tensor", sq.t[:, 0:N], o.t[:, 0:N], o.t[:, 0:N], op=ALU.mult)
        ms = PS[0]
        fw.op(PE, [ones, sq], [ms], "matmul", ms.t[:, 0:N], ones.t, sq.t[:, 0:N], start=True, stop=True)
        fw.op(ACT, [ms, epsc], [r0], "activation", r0.t[:, 0:N], ms.t[:, 0:N], AF.Ln, bias=epsc.t, scale=1.0 / 128.0)
        fw.op(ACT, [r0], [r0], "activation", r0.t[:, 0:N], r0.t[:, 0:N], AF.Exp, scale=-0.5)
        fw.op(DVE, [o, gcol, r0], [ob], "scalar_tensor_tensor", ob.t[:, 0:N], o.t[:, 0:N], gcol.t, r0.t[:, 0:N], op0=ALU.mult, op1=ALU.mult)
        fw.dma(odst, ob.t[:, 0:N].rearrange("p (g t) -> p g t", t=128) if N == 512 else ob.t[:, 0:N], [ob], [oreg], ob)

    zb = (zcol, zcol.t)
    k0b = (kb0t, kb0t.t)

    def prompt_units(j, desc):
        nkb = 8 * (j + 1)
        order = range(nkb - 1, -1, -1) if desc else range(nkb)
        us = []
        for kb in order:
            r = kb - (8 * j + 4)
            if r >= 0:
                us.append((kb, 128, MSK, MSKv[:, r, :], zb))
            elif kb < 4:
                us.append((kb, 128, None, None, k0b))
            else:
                us.append((kb, 128, None, None, zb))
        return us

    def smp_units(desc, masked):
        us = []
        order = range(NKS - 1, -1, -1) if desc else range(NKS)
        for kb in order:
            if kb == NKS - 1:
                us.append((kb, 64, MSMP if masked else None, MSMP.t[0:64, :] if masked else None, zb))
            else:
                us.append((kb, 128, None, None, zb))
        return us

    fw.op(POOL, [], [H2[5]], "memset", H2[5].t, 0.0)
    fw.op(POOL, [], [H2[6]], "memset", H2[6].t, 0.0)
    fw.op(POOL, [], [F4[2]], "memset", F4[2].t, 0.0)
    fw.op(POOL, [], [F4[3]], "memset", F4[3].t, 0.0)
    segs = []
    for ch in range(8):
        segs.append((ch, "p"))
        segs.append((ch, "s"))

    def seg_load(i):
        ch, kind = segs[i]
        vcol = ch * 128
        if kind == "p":
            Kb, Ka, Vb, Va = kvset("p")
            load_kv(Kb, Ka, Vb, Va, KT[ch], fw.reg("KT"), VS, fw.reg("VS"), T // 128, vcol)
        else:
            for be in range(2):
                Kb, Ka, Vb, Va = kvset("s", be)
                load_kv(Kb, Ka, Vb, Va, KTS[be, ch], fw.reg(("KTS", be)), VSS[be], fw.reg(("VSS", be)), NKS, vcol)

    sbs = []
    for ch in range(4):
        for j in range(8):
            q0 = (2 * j + 1) * 512
            sbs.append((QT[ch, :, q0:q0 + 512], 512))
        for be in range(2):
            sbs.append((QT[ch, :, T + be * 64:T + be * 64 + 64], 64))
    sbi = {"i": 0}

    dfs = []
    for ch in range(4, 8):
        for j in range(8):
            q0 = (2 * j + 1) * 512
            dfs.append((QT[ch, :, q0:q0 + 512], 512))
        for be in range(2):
            dfs.append((QT[ch, :, T + be * 64:T + be * 64 + 64], 64))
    dfi = {"i": 0}

    def run_df(units, N, odst):
        i = dfi["i"]
        if i == 0:
            load_q(H2[5], H2[5].t, dfs[0][0], fw.reg("QT"), dfs[0][1])
        if i + 1 < len(dfs):
            qb = H2[5 + (i + 1) % 2]
            load_q(qb, qb.t, dfs[i + 1][0], fw.reg("QT"), dfs[i + 1][1])
        df_slot(i % 2, N, units, odst, fw.reg("OT"))
        dfi["i"] = i + 1

    def run_sb(units, N, odst):
        i = sbi["i"]
        if i + 1 < len(sbs):
            load_q_sb((i + 1) % 2, sbs[i + 1][0], fw.reg("QT"), sbs[i + 1][1])
        sb_slot(i % 2, N, units, odst, fw.reg("OT"))
        sbi["i"] = i + 1

    def seg_run(i):
        ch, kind = segs[i]
        issb = ch < 4
        if kind == "p":
            cur["kv"] = kvset("p")
            if ch in (0, 4):
                fw.dma(MSKv, msbd if issb else mdfd, [], [MSK], MSK)
            for j in range(8):
                q0 = (2 * j + 1) * 512
                od = OT[4 * j:4 * j + 4, :, ch, :].rearrange("g p t -> p g t")
                if issb:
                    run_sb(prompt_units(j, True), 512, od)
                else:
                    run_df(prompt_units(j, False), 512, od)
        else:
            for be in range(2):
                cur["kv"] = kvset("s", be)
                qs = QT[ch, :, T + be * 64:T + be * 64 + 64]
                od = OT[NG // 2, :, ch, be * 64:(be + 1) * 64]
                if issb:
                    run_sb(smp_units(True, True), 64, od)
                else:
                    run_df(smp_units(False, False), 64, od)

    load_q_sb(0, sbs[0][0], fw.reg("QT"), sbs[0][1])
    seg_load(0)
    for i in range(len(segs)):
        if i + 1 < len(segs):
            seg_load(i + 1)
        seg_run(i)

    for c in range(8):
        for q in range(2):
            load_cast(Wd, Wd.t[:, c, q * 512:(q + 1) * 512], wo[c * 128:(c + 1) * 128, q * 512:(q + 1) * 512], 512)
    load_ln(2)
    NGD = NG // 2 + 1

    def x1rows(g):
        if g < NG // 2:
            s = 2 * (g // 4) + 1
            r0 = s * 512 + (g % 4) * 128
            return X1[r0:r0 + 128, :], fw.reg(("X1", r0 // 128))
        return X1[T:T + 128, :], fw.reg(("X1", NG))

    LNS = []
    for k in range(3):
        LNS.append((sb("stats%d" % k, [128, 2, 6], F32), sb("mv%d" % k, [128, 2], F32), sb("lnv%d" % k, [128, 1], F32), sb("rstd%d" % k, [128, 1], F32)))

    def ldD(g):
        og = H2[g % 3]; rt = F4[g % 4]
        fw.dma(og.t, OT[g].rearrange("p c t -> p (c t)"), [fw.reg("OT")], [og], og)
        ap, rb = x1rows(g)
        fw.dma(rt.t, ap, [rb], [rt], rt)

    def D1a(g):
        og = H2[g % 3]; rt = F4[g % 4]
        st_, mv_, lnv_, rstd_ = LNS[g % 3]
        for half in range(2):
            for c in range(8):
                fw.op(PE, [og, Wd], [PS[half]], "matmul", PS[half].t, og.t[:, c * 128:(c + 1) * 128], Wd.t[:, c, half * 512:(half + 1) * 512], start=(c == 0), stop=(c == 7), signal=(c == 7))
        for half in range(2):
            fw.op(DVE, [PS[half], rt], [rt], "scalar_tensor_tensor", rt.t[:, half * 512:(half + 1) * 512], PS[half].t, 1.0 / ALPHA, rt.t[:, half * 512:(half + 1) * 512], op0=ALU.mult, op1=ALU.add)
        for h in range(2):
            fw.op(DVE, [rt], [st_], "bn_stats", st_.t[:, h, :], rt.t[:, h * 512:(h + 1) * 512])
        fw.op(DVE, [st_], [mv_], "bn_aggr", mv_.t, st_.t)

    def D1b(g):
        st_, mv_, lnv_, rstd_ = LNS[g % 3]
        fw.op(ACT, [mv_, epsl], [lnv_], "activation", lnv_.t, mv_.t[:, 1:2], AF.Ln, bias=epsl.t, scale=1.0)
        fw.op(ACT, [lnv_], [rstd_], "activation", rstd_.t, lnv_.t, AF.Exp, scale=-0.5)

    def D1c(g):
        rt = F4[g % 4]
        st_, mv_, lnv_, rstd_ = LNS[g % 3]
        fw.op(DVE, [rt, mv_, rstd_], [rt], "tensor_scalar", rt.t, rt.t, mv_.t[:, 0:1], rstd_.t, op0=ALU.subtract, op1=ALU.mult)
        fw.op(POOL, [rt, F4[4]], [rt], "tensor_tensor", rt.t, rt.t, F4[4].t, op=ALU.mult)
        fw.op(POOL, [rt, F4[5]], [rt], "tensor_tensor", rt.t, rt.t, F4[5].t, op=ALU.add)
        fw.dma(X2[g * 128:(g + 1) * 128, :], rt.t, [rt], [fw.reg(("X2", g))], rt)

    ldD(0)
    ldD(1)
    for g in range(NGD + 2):
        if g < NGD:
            D1a(g)
        if 0 <= g - 1 < NGD:
            D1b(g - 1)
        if 0 <= g - 2 < NGD:
            D1c(g - 2)
        if g + 2 < NGD:
            ldD(g + 2)

    def srcD(g):
        return X2[g * 128:(g + 1) * 128, :], [fw.reg(("X2", g))]

    def dstD(g, rt):
        if g < NG // 2:
            fw.dma(yp[g * 128:(g + 1) * 128, :], rt.t, [rt], [], rt, is_output=True)
        else:
            fw.dma(ys, rt.t, [rt], [], rt, is_output=True)

    ffn_stage(NGD, srcD, f2, 4, dstD, False)
    fw.finish()
    return nc


def _consts():
    bf = ml_dtypes.bfloat16
    p = np.arange(128)[:, None]
    ident = (p == np.arange(128)[None, :]).astype(np.float32).astype(bf)
    uinc = (p >= np.arange(128)[None, :]).astype(np.float32).astype(bf)
    t = np.arange(512)[None, None, :]
    r = np.arange(4)[None, :, None]
    s = 128 * r + p[:, :, None]
    msb = (s < t).astype(np.float32).astype(bf)
    mdf = ((s // 64) <= (t // 64)).astype(np.float32).astype(bf)
    msmp = (p < np.arange(64)[None, :]).astype(np.float32).astype(bf)
    return ident, uinc, np.ascontiguousarray(msb), np.ascontiguousarray(mdf), msmp


def _rope_tab(pos):
    half = 32
    inv = (10000.0 ** (-np.arange(half, dtype=np.float32) / half)).astype(np.float32)
    ang = pos.astype(np.float32)[:, None] * inv[None, :]
    return np.cos(ang).astype(np.float32), np.sin(ang).astype(np.float32)


def kernel(**inp):
    f = lambda k: np.ascontiguousarray(np.asarray(inp[k], dtype=np.float32))
    x_prompt = f("x_prompt"); x_sample = f("x_sample")
    ident, uinc, msb, mdf, msmp = _consts()
    nc = build()
    coss, sins = _rope_tab(PAST + np.concatenate([np.arange(64), np.arange(64)]))
    shared = {
        "f1g": f("ffn1_wg")[0], "f1u": f("ffn1_wu")[0], "f1d": f("ffn1_wd")[0],
        "f2g": f("ffn2_wg")[0], "f2u": f("ffn2_wu")[0], "f2d": f("ffn2_wd")[0],
        "win": f("w_in")[0], "wo": f("w_o")[0],
        "ln1g": f("ln1_g"), "ln1b": f("ln1_b"), "ln2g": f("ln2_g"), "ln2b": f("ln2_b"), "ln3g": f("ln3_g"), "ln3b": f("ln3_b"),
        "lq1": f("lambda_q1"), "lk1": f("lambda_k1"), "lq2": f("lambda_q2"), "lk2": f("lambda_k2"),
        "subg": np.ascontiguousarray(f("subln_g").reshape(128, 1)),
        "coss": coss, "sins": sins, "ident": ident, "uinc": uinc, "msb": msb, "mdf": mdf, "msmp": msmp,
    }
    csk = f("cache_sb_k")[0].reshape(16, PAST, 512); csv = f("cache_sb_v")[0].reshape(16, PAST, 512)
    cdk = f("cache_diff_k")[0].reshape(16, PAST, 512); cdv = f("cache_diff_v")[0].reshape(16, PAST, 512)
    in_maps = []
    for c in range(8):
        b, h = c // 2, c % 2
        if h == 0:
            xpc = np.concatenate([np.zeros((512, D), np.float32), x_prompt[b, :T - 512]], axis=0)
            pos = np.arange(T) - 512
        else:
            xpc = x_prompt[b]
            pos = np.arange(T)
        cp, sp = _rope_tab(np.maximum(pos, 0))
        m = dict(shared)
        m.update({
            "xp": np.ascontiguousarray(xpc), "xs": np.ascontiguousarray(x_sample[2 * c:2 * c + 2].reshape(128, D)),
            "c_sbk": np.ascontiguousarray(csk[2 * c:2 * c + 2]), "c_sbv": np.ascontiguousarray(csv[2 * c:2 * c + 2]),
            "c_dfk": np.ascontiguousarray(cdk[2 * c:2 * c + 2]), "c_dfv": np.ascontiguousarray(cdv[2 * c:2 * c + 2]),
            "cosp": cp, "sinp": sp,
            "kb0": np.full((128, 1), NEGBIG if h == 0 else 0.0, np.float32),
        })
        in_maps.append(m)
    res = run_bass_kernel_spmd(nc, in_maps, core_ids=list(range(8)))
    R = res.results
    y_p = np.zeros((4, T, D), np.float32)
    y_s = np.zeros((16, 64, D), np.float32)
    kvp = [np.zeros((1, 4, T, 512), np.float32) for _ in range(4)]
    kvs = [np.zeros((1, 16, 64, 512), np.float32) for _ in range(4)]
    names_p = ("o_sbk", "o_sbv", "o_dfk", "o_dfv"); names_s = ("s_sbk", "s_sbv", "s_dfk", "s_dfv")
    for c in range(8):
        b, h = c // 2, c % 2
        yo = R[c]["yp"].reshape(8, 512, D)
        for j in range(8):
            sblk = 2 * j + h
            y_p[b, sblk * 512:(sblk + 1) * 512] = yo[j]
        y_s[2 * c:2 * c + 2] = R[c]["ys"].reshape(2, 64, D)
        for i in range(4):
            kvs[i][0, 2 * c:2 * c + 2] = R[c][names_s[i]].reshape(2, 64, 512)
            if h == 1:
                kvp[i][0, b] = R[c][names_p[i]]
    return (y_p, y_s,
            kvp[0].reshape(1, 4, T, 8, 64), kvp[1].reshape(1, 4, T, 8, 64),
            kvp[2].reshape(1, 4, T, 4, 2, 64), kvp[3].reshape(1, 4, T, 4, 128),
            kvs[0].reshape(1, 16, 64, 8, 64), kvs[1].reshape(1, 16, 64, 8, 64),
            kvs[2].reshape(1, 16, 64, 4, 2, 64), kvs[3].reshape(1, 16, 64, 4, 128))
```

```python
import math
import numpy as np
import ml_dtypes
import concourse.bass as bass
import concourse.mybir as mybir
from concourse.bass_utils import run_bass_kernel_spmd

F32 = mybir.dt.float32
BF16 = mybir.dt.bfloat16
AF = mybir.ActivationFunctionType
ALU = mybir.AluOpType
AX = mybir.AxisListType

D = 1024
DFF = 2816
NFC = 22
T = 8192
NG = 64
PAST = 4096
NKS = 33
TKS = 4224
ALPHA = 2.0 ** 0.25
EPS = 1e-5
LAM0 = 0.8 - 0.6 * math.exp(0.0)
NEGBIG = -30000.0
SAME_SYNC = True


class Buf:
    def __init__(self, t=None, name=""):
        self.t = t
        self.name = name
        self.lw = {}
        self.rd = {}
        self.dsem = None
        self.dn = 0


class Eng:
    def __init__(self, fw, name, is_pe=False):
        self.name = name
        self.is_pe = is_pe
        self.sem = fw.nc.alloc_semaphore("sem_" + name)
        self.n = 0
        self.waited = {}
        self.prog = []

    def wait(self, ev):
        sem, val = ev
        if sem is self.sem:
            if self.is_pe or not SAME_SYNC:
                return
            assert val <= self.n
        k = id(sem)
        if self.waited.get(k, 0) >= val:
            return
        self.prog.append(("wait", sem, val))
        self.waited[k] = val


class FW:
    def __init__(self, nc):
        self.nc = nc
        self.pe = Eng(self, "pe", True)
        self.act = Eng(self, "act")
        self.dve = Eng(self, "dve")
        self.pool = Eng(self, "pool")
        self.sp = Eng(self, "sp")
        self.out_events = []
        self.regs = {}
        self.ndsem = 0

    def sb(self, name, shape, dtype):
        return Buf(self.nc.alloc_sbuf_tensor(name, list(shape), dtype).ap(), name)

    def ps(self, name, shape, dtype=F32):
        return Buf(self.nc.alloc_psum_tensor(name, list(shape), dtype).ap(), name)

    def reg(self, key):
        b = self.regs.get(key)
        if b is None:
            b = Buf(None, str(key))
            self.regs[key] = b
        return b

    def _deps(self, E, reads, writes):
        for b in reads:
            for ev in b.lw.values():
                E.wait(ev)
        for b in writes:
            for ev in b.lw.values():
                E.wait(ev)
            for ev in b.rd.values():
                E.wait(ev)

    def _record(self, ev, reads, writes):
        k = id(ev[0])
        for b in writes:
            if b.t is None:
                b.lw[k] = ev
            else:
                b.lw = {k: ev}
                b.rd = {}
        for b in reads:
            if b in writes:
                continue
            old = b.rd.get(k)
            if old is None or old[1] < ev[1]:
                b.rd[k] = ev

    def op(self, E, reads, writes, meth, *args, signal=True, **kw):
        self._deps(E, reads, writes)
        if signal:
            E.prog.append(("op", meth, args, kw, E.sem))
            E.n += 1
            ev = (E.sem, E.n)
        else:
            E.prog.append(("op", meth, args, kw, None))
            ev = (E.sem, E.n + 1)
        self._record(ev, reads, writes)

    def dma(self, out_ap, in_ap, reads, writes, owner, is_output=False, chain=False, q=None):
        Q = self.sp if q is None else q
        if owner.dsem is None:
            owner.dsem = self.nc.alloc_semaphore("d%d" % self.ndsem)
            self.ndsem += 1
        self._deps(Q, reads, writes)
        if owner.dn > 0 and not chain:
            Q.wait((owner.dsem, 16 * owner.dn))
        Q.prog.append(("dma", out_ap, in_ap, owner.dsem))
        owner.dn += 1
        ev = (owner.dsem, 16 * owner.dn)
        self._record(ev, reads, writes)
        if is_output:
            self.out_events.append(ev)

    def finish(self):
        last = {}
        for ev in self.out_events:
            k = id(ev[0])
            if k not in last or last[k][1] < ev[1]:
                last[k] = ev
        for ev in last.values():
            self.sp.wait(ev)
        nc = self.nc
        with nc.Block() as block:
            for E, dec in ((self.sp, block.sync), (self.pe, block.tensor), (self.act, block.scalar),
                           (self.dve, block.vector), (self.pool, block.gpsimd)):
                def body(e, E=E):
                    for it in E.prog:
                        if it[0] == "wait":
                            e.wait_ge(it[1], it[2])
                        elif it[0] == "dma":
                            e.dma_start(out=it[1], in_=it[2]).then_inc(it[3], 16)
                        else:
                            ins = getattr(e, it[1])(*it[2], **it[3])
                            if it[4] is not None:
                                ins.then_inc(it[4], 1)
                dec(body)


def build():
    nc = bass.Bass("TRN2", target_bir_lowering=False)

    def din(name, shape, dt=F32):
        return nc.dram_tensor(name, list(shape), dt, kind="ExternalInput").ap()

    def dout(name, shape):
        return nc.dram_tensor(name, list(shape), F32, kind="ExternalOutput").ap()

    def dscr(name, shape, dt):
        return nc.dram_tensor(name, list(shape), dt, kind="Internal").ap()

    xp = din("xp", [T, D]); xs = din("xs", [128, D])
    c_k = [din("c_sbk", [2, PAST, 512]), din("c_dfk", [2, PAST, 512])]
    c_v = [din("c_sbv", [2, PAST, 512]), din("c_dfv", [2, PAST, 512])]
    f1 = (din("f1g", [D, DFF]), din("f1u", [D, DFF]), din("f1d", [DFF, D]))
    f2 = (din("f2g", [D, DFF]), din("f2u", [D, DFF]), din("f2d", [DFF, D]))
    win = din("win", [D, 3 * D]); wo = din("wo", [D, D])
    lnp = [din(n, [1, D]) for n in ("ln1g", "ln1b", "ln2g", "ln2b", "ln3g", "ln3b")]
    lamv = [din(n, [1, 64]) for n in ("lq1", "lk1", "lq2", "lk2")]
    subg = din("subg", [128, 1])
    cosp = din("cosp", [T, 32]); sinp = din("sinp", [T, 32])
    coss = din("coss", [128, 32]); sins = din("sins", [128, 32])
    kb0 = din("kb0", [128, 1])
    identd = din("ident", [128, 128], BF16); uincd = din("uinc", [128, 128], BF16)
    msbd = din("msb", [128, 4, 512], BF16); mdfd = din("mdf", [128, 4, 512], BF16)
    msmpd = din("msmp", [128, 64], BF16)

    yp = dout("yp", [T // 2, D]); ys = dout("ys", [128, D])
    o_kv = [dout(n, [T, 512]) for n in ("o_sbk", "o_sbv", "o_dfk", "o_dfv")]
    s_kv = [dout(n, [128, 512]) for n in ("s_sbk", "s_sbv", "s_dfk", "s_dfv")]

    X1 = dscr("X1", [T + 128, D], F32)
    X1T = dscr("X1T", [NG + 1, 128, D], BF16)
    QT = dscr("QT", [8, 128, T + 128], BF16)
    KT = dscr("KT", [8, 128, T], BF16)
    KTS = dscr("KTS", [2, 8, 128, TKS], BF16)
    VS = dscr("VS", [T, D], BF16)
    VSS = dscr("VSS", [2, TKS, D], BF16)
    OT = dscr("OT", [NG // 2 + 1, 128, 8, 128], BF16)
    X2 = dscr("X2", [T // 2 + 128, D], F32)

    fw = FW(nc)
    PE, ACT, DVE, POOL = fw.pe, fw.act, fw.dve, fw.pool
    sb = fw.sb
    Wg = sb("Wg", [128, 8, DFF], BF16); Wu = sb("Wu", [128, 8, DFF], BF16); Wd = sb("Wd", [128, NFC, D], BF16)
    F4 = [sb("F4_%d" % i, [128, D], F32) for i in range(6)]
    H2 = [sb("H2_%d" % i, [128, D], BF16) for i in range(8)]
    ST = [sb("ST_%d" % i, [128, 704], F32) for i in range(2)]
    HTB = sb("HTB", [128, DFF], BF16)
    MSK = F4[5]
    MSKv = F4[5].t.bitcast(BF16).rearrange("p (r n) -> p r n", n=512)
    MSMP = sb("MSMP", [128, 64], BF16)
    ident = sb("identb", [128, 128], BF16); uinc = sb("uincb", [128, 128], BF16); ones = sb("onesb", [128, 128], BF16)
    kb0t = sb("kb0t", [128, 1], F32); zcol = sb("zcol", [128, 1], F32); epsc = sb("epsc", [128, 1], F32)
    gcol = sb("gcol", [128, 1], F32); nlam = sb("nlam", [128, 1], F32)
    lamt = sb("lamt", [128, 4, 64], F32); lamp = sb("lamp", [128, 2, 64], F32); lams = sb("lams", [128, 2], F32)
    stats = sb("stats", [128, 2, 6], F32); mv = sb("mv", [128, 2], F32); lnv = sb("lnv", [128, 1], F32); rstd = sb("rstd", [128, 1], F32)
    cst = sb("cst", [128, 2, 32], F32)
    RT = [sb("RT_%d" % i, [128, 8, 32], F32) for i in range(2)]
    PSALL = nc.alloc_psum_tensor("PSALL", [128, 4096], F32).ap()
    PS = [Buf(PSALL[:, i * 512:(i + 1) * 512], "PS%d" % i) for i in range(8)]
    ones32 = sb("ones32", [128, 128], F32)

    def psbf(b):
        return b.t.bitcast(BF16)

    def pbc(ap1):
        return ap1.partition_broadcast(128)

    fw.dma(ident.t, identd, [], [ident], ident)
    fw.dma(uinc.t, uincd, [], [uinc], uinc)
    fw.dma(kb0t.t, kb0, [], [kb0t], kb0t)
    fw.dma(gcol.t, subg, [], [gcol], gcol)
    fw.dma(MSMP.t, msmpd, [], [MSMP], MSMP)
    for i in range(4):
        fw.dma(lamt.t[:, i:i + 1, :], pbc(lamv[i]), [], [lamt], lamt)
    fw.op(POOL, [], [ones], "memset", ones.t, 1.0)
    fw.op(POOL, [], [ones32], "memset", ones32.t, 1.0)
    fw.op(POOL, [], [zcol], "memset", zcol.t, 0.0)
    fw.op(POOL, [], [epsc], "memset", epsc.t, EPS)
    epsl = sb("epsl", [128, 1], F32)
    fw.op(POOL, [], [epsl], "memset", epsl.t, EPS / (ALPHA * ALPHA))
    fw.op(DVE, [gcol], [gcol], "tensor_scalar", gcol.t, gcol.t, 1.0 - LAM0, None, op0=ALU.mult)
    fw.op(DVE, [lamt], [lamp], "tensor_tensor", lamp.t[:, 0, :], lamt.t[:, 0, :], lamt.t[:, 1, :], op=ALU.mult)
    fw.op(DVE, [lamt], [lamp], "tensor_tensor", lamp.t[:, 1, :], lamt.t[:, 2, :], lamt.t[:, 3, :], op=ALU.mult)
    fw.op(DVE, [lamp], [lams], "reduce_sum", lams.t, lamp.t, axis=AX.X)
    fw.op(ACT, [lams], [lams], "activation", lams.t, lams.t, AF.Exp)
    fw.op(DVE, [lams], [nlam], "tensor_tensor", nlam.t, lams.t[:, 1:2], lams.t[:, 0:1], op=ALU.subtract)
    fw.op(DVE, [nlam], [nlam], "tensor_scalar", nlam.t, nlam.t, -LAM0, None, op0=ALU.add)

    cnt = {"st": 0}

    def load_cast(dstb, dst_ap, src_ap, n):
        k = cnt["st"]; cnt["st"] += 1
        st = ST[k % 2]
        fw.dma(st.t[:, 0:n], src_ap, [], [st], st)
        E = DVE if k % 2 == 0 else POOL
        fw.op(E, [st], [dstb], "tensor_copy", dst_ap, st.t[:, 0:n])

    def ffn_w_chunks(ws):
        ch = []
        for dc in range(8):
            for q in range(4):
                ch.append((Wg, Wg.t[:, dc, q * 704:(q + 1) * 704], ws[0][dc * 128:(dc + 1) * 128, q * 704:(q + 1) * 704], 704))
        for dc in range(8):
            for q in range(4):
                ch.append((Wu, Wu.t[:, dc, q * 704:(q + 1) * 704], ws[1][dc * 128:(dc + 1) * 128, q * 704:(q + 1) * 704], 704))
        for fc in range(NFC):
            for q in range(2):
                ch.append((Wd, Wd.t[:, fc, q * 512:(q + 1) * 512], ws[2][fc * 128:(fc + 1) * 128, q * 512:(q + 1) * 512], 512))
        return ch

    def load_ffn_w(ws, skip=0):
        for c in ffn_w_chunks(ws)[skip:]:
            load_cast(*c)

    def ln_stats(rt):
        for h in range(2):
            fw.op(DVE, [rt], [stats], "bn_stats", stats.t[:, h, :], rt.t[:, h * 512:(h + 1) * 512])
        fw.op(DVE, [stats], [mv], "bn_aggr", mv.t, stats.t)

    def ln_act():
        fw.op(ACT, [mv, epsl], [lnv], "activation", lnv.t, mv.t[:, 1:2], AF.Ln, bias=epsl.t, scale=1.0)
        fw.op(ACT, [lnv], [rstd], "activation", rstd.t, lnv.t, AF.Exp, scale=-0.5)

    def ln_apply(rt):
        fw.op(DVE, [rt, mv, rstd], [rt], "tensor_scalar", rt.t, rt.t, mv.t[:, 0:1], rstd.t, op0=ALU.subtract, op1=ALU.mult)
        fw.op(POOL, [rt, F4[4]], [rt], "tensor_tensor", rt.t, rt.t, F4[4].t, op=ALU.mult)
        fw.op(POOL, [rt, F4[5]], [rt], "tensor_tensor", rt.t, rt.t, F4[5].t, op=ALU.add)

    def layernorm(rt):
        ln_stats(rt)
        ln_act()
        ln_apply(rt)

    def load_ln(gi):
        fw.dma(F4[4].t.unsqueeze(1), pbc(lnp[gi]), [], [F4[4]], F4[4])
        fw.dma(F4[5].t.unsqueeze(1), pbc(lnp[gi + 1]), [], [F4[5]], F4[5])

    def transpose8(src, psb, dst, dst_ap=None):
        pb = psbf(psb)
        for c in range(8):
            fw.op(PE, [src, ident], [psb], "transpose", pb[:, c * 128:(c + 1) * 128], src.t[:, c * 128:(c + 1) * 128], ident.t, signal=(c == 7))
        fw.op(ACT, [psb], [dst], "copy", dst.t if dst_ap is None else dst_ap, pb)

    def ffn_stage(ngroups, src_fn, ws, lni, dst_fn, x1t, skip=0):
        load_ffn_w(ws, skip)
        load_ln(lni)
        nfb = (DFF + 511) // 512
        HB = []
        for fb_ in range(nfb):
            par = H2[5 + fb_ // 2]
            hbuf = Buf(par.t[:, (fb_ % 2) * 512:(fb_ % 2) * 512 + min(512, DFF - fb_ * 512)], "HB%d" % fb_)
            hbuf.lw = dict(par.lw); hbuf.rd = dict(par.rd)
            HB.append(hbuf)

        def ld(g):
            xt = F4[g % 2]
            ap, rb = src_fn(g)
            fw.dma(xt.t, ap, rb, [xt], xt)

        def PRE_cast(g):
            xt = F4[g % 2]
            xbf = H2[0]
            fw.op(ACT, [xt], [xbf], "copy", xbf.t, xt.t)
            if g + 1 < ngroups:
                ld(g + 1)

        def PRE_tr(g):
            transpose8(H2[0], PS[7], H2[1 + g % 2])

        def PRE(g):
            PRE_cast(g)
            PRE_tr(g)

        def ldr(g):
            rt = F4[2 + g % 2]
            ap, rb = src_fn(g)
            fw.dma(rt.t, ap, rb, [rt], rt)

        def GU(g, fb):
            xT = H2[1 + g % 2]
            f0 = fb * 512
            fn = min(512, DFF - f0)
            pg = PS[fb % 2]; pu = PS[2 + fb % 2]
            for dc in range(8):
                fw.op(PE, [Wg, xT], [pg], "matmul", pg.t[:, 0:fn], xT.t[:, dc * 128:(dc + 1) * 128], Wg.t[:, dc, f0:f0 + fn], start=(dc == 0), stop=(dc == 7), signal=(dc == 7))
            for dc in range(8):
                fw.op(PE, [Wu, xT], [pu], "matmul", pu.t[:, 0:fn], xT.t[:, dc * 128:(dc + 1) * 128], Wu.t[:, dc, f0:f0 + fn], start=(dc == 0), stop=(dc == 7), signal=(dc == 7))
            sg = ST[fb % 2]
            hb = HB[fb]
            fw.op(ACT, [pg], [sg], "activation", sg.t[:, 0:fn], pg.t[:, 0:fn], AF.Silu)
            fw.op(DVE, [sg, pu], [hb], "tensor_tensor", hb.t[:, 0:fn], sg.t[:, 0:fn], pu.t[:, 0:fn], op=ALU.mult)

        def TR(g, fb):
            f0 = fb * 512
            fn = min(512, DFF - f0)
            hb = HB[fb]
            hc = 0
            pb = PS[6 + fb % 2]
            pbv = psbf(pb)
            nch = fn // 128
            for c in range(nch):
                fw.op(PE, [hb, ident], [pb], "transpose", pbv[:, c * 128:(c + 1) * 128], hb.t[:, hc + c * 128:hc + (c + 1) * 128], ident.t, signal=(c == nch - 1))
            fw.op(ACT, [pb], [HTB], "copy", HTB.t[:, f0:f0 + fn], pbv[:, 0:fn])

        def DOWN(g):
            for half in range(2):
                for fc in range(NFC):
                    fw.op(PE, [HTB, Wd], [PS[4 + half]], "matmul", PS[4 + half].t, HTB.t[:, fc * 128:(fc + 1) * 128], Wd.t[:, fc, half * 512:(half + 1) * 512], start=(fc == 0), stop=(fc == NFC - 1), signal=(fc == NFC - 1))

        def POST_a(g):
            rt = F4[2 + g % 2]
            for half in range(2):
                fw.op(DVE, [PS[4 + half], rt], [rt], "scalar_tensor_tensor", rt.t[:, half * 512:(half + 1) * 512], PS[4 + half].t, 0.5 / ALPHA, rt.t[:, half * 512:(half + 1) * 512], op0=ALU.mult, op1=ALU.add)
            ln_stats(rt)

        def POST_b(g):
            rt = F4[2 + g % 2]
            ln_apply(rt)
            dst_fn(g, rt)
            if x1t:
                x1bf = H2[3]
                fw.op(POOL, [rt], [x1bf], "tensor_copy", x1bf.t, rt.t)

        def X1TR(g):
            x1bf = H2[3]
            x1T = H2[4]
            transpose8(x1bf, PS[7], x1T)
            r = fw.reg(("X1T", g))
            fw.dma(X1T[g], x1T.t, [x1T], [r], x1T)

        ld(0)
        PRE(0)
        for g in range(ngroups):
            for fb in range(nfb):
                GU(g, fb)
                if fb >= 1:
                    TR(g, fb - 1)
                if fb == 0:
                    if g > 0:
                        TR(g - 1, nfb - 1)
                        DOWN(g - 1)
                    ldr(g)
                if fb == 1:
                    if g > 0:
                        POST_a(g - 1)
                    if g + 1 < ngroups:
                        PRE_cast(g + 1)
                    if g > 1 and x1t:
                        X1TR(g - 2)
                if fb == 2:
                    if g + 1 < ngroups:
                        PRE_tr(g + 1)
                    if g > 0:
                        ln_act()
                if fb == 3 and g > 0:
                    POST_b(g - 1)
        TR(ngroups - 1, nfb - 1)
        DOWN(ngroups - 1)
        POST_a(ngroups - 1)
        ln_act()
        if x1t and ngroups > 1:
            X1TR(ngroups - 2)
        POST_b(ngroups - 1)
        for fb_ in range(nfb):
            par = H2[5 + fb_ // 2]
            for d_src, d_dst in ((HB[fb_].lw, par.lw), (HB[fb_].rd, par.rd)):
                for k, ev in d_src.items():
                    if k not in d_dst or d_dst[k][1] < ev[1]:
                        d_dst[k] = ev
        if x1t:
            X1TR(ngroups - 1)

    def srcA(g):
        if g < NG:
            return xp[g * 128:(g + 1) * 128, :], []
        return xs, []

    def dstA(g, rt):
        r = fw.reg(("X1", g))
        fw.dma(X1[g * 128:(g + 1) * 128, :], rt.t, [rt], [r], rt)

    ffn_stage(NG + 1, srcA, f1, 0, dstA, True)

    for dc in range(8):
        for cb in range(6):
            dstb = Wg if cb < 4 else Wu
            c0 = (cb % 4) * 512
            load_cast(dstb, dstb.t[:, dc, c0:c0 + 512], win[dc * 128:(dc + 1) * 128, cb * 512:(cb + 1) * 512], 512)

    def wview(dc, cb):
        b = Wg if cb < 4 else Wu
        c0 = (cb % 4) * 512
        return b, b.t[:, dc, c0:c0 + 512]

    def rope(psb, outb, out_ap):
        x = psb.t.rearrange("p (a b) -> p a b", b=64)
        o = out_ap.rearrange("p (a b) -> p a b", b=64)
        cb_ = cst.t[:, 0:1, :].to_broadcast([128, 8, 32]); sb_ = cst.t[:, 1:2, :].to_broadcast([128, 8, 32])
        tA, tB = RT
        fw.op(DVE, [psb, cst], [tA], "tensor_tensor", tA.t, x[:, :, 0:32], cb_, op=ALU.mult)
        fw.op(DVE, [psb, cst], [tB], "tensor_tensor", tB.t, x[:, :, 32:64], sb_, op=ALU.mult)
        fw.op(DVE, [tA, tB], [outb], "tensor_tensor", o[:, :, 0:32], tA.t, tB.t, op=ALU.subtract)
        fw.op(DVE, [psb, cst], [tA], "tensor_tensor", tA.t, x[:, :, 32:64], cb_, op=ALU.mult)
        fw.op(DVE, [psb, cst], [tB], "tensor_tensor", tB.t, x[:, :, 0:32], sb_, op=ALU.mult)
        fw.op(DVE, [tA, tB], [outb], "tensor_tensor", o[:, :, 32:64], tA.t, tB.t, op=ALU.add)

    bcnt = {"n": 0}

    def ldB(g):
        xt = H2[g % 2]
        fw.dma(xt.t, X1T[g], [fw.reg(("X1T", g))], [xt], xt)
    ldB(0)
    for g in range(NG + 1):
        if g + 1 < NG + 1:
            ldB(g + 1)
        x1T = H2[g % 2]
        smp = g == NG
        if smp:
            fw.dma(cst.t[:, 0, :], coss, [], [cst], cst); fw.dma(cst.t[:, 1, :], sins, [], [cst], cst)
        else:
            fw.dma(cst.t[:, 0, :], cosp[g * 128:(g + 1) * 128, :], [], [cst], cst)
            fw.dma(cst.t[:, 1, :], sinp[g * 128:(g + 1) * 128, :], [], [cst], cst)
        qb, kb, vb = H2[2], H2[3], H2[4]
        qT, kT = H2[5], H2[6]

        def blk(cb):
            k_ = bcnt["n"]; bcnt["n"] += 1
            pb_ = PS[k_ % 4]
            for dc in range(8):
                wb, wap = wview(dc, cb)
                fw.op(PE, [x1T, wb], [pb_], "matmul", pb_.t, x1T.t[:, dc * 128:(dc + 1) * 128], wap, start=(dc == 0), stop=(dc == 7), signal=(dc == 7))
            fo = F4[k_ % 4]
            foa = fo.t[:, 0:512]
            if cb == 0:
                fw.op(ACT, [pb_], [qb], "activation", qb.t[:, 0:512], pb_.t, AF.Identity, scale=0.125)
            elif cb == 3:
                rope(pb_, fo, foa)
                fw.op(ACT, [fo], [qb], "activation", qb.t[:, 512:1024], foa, AF.Identity, scale=0.125)
            else:
                if cb == 4:
                    rope(pb_, fo, foa)
                else:
                    fw.op(ACT, [pb_], [fo], "copy", foa, pb_.t)
                oi = {1: 0, 2: 1, 4: 2, 5: 3}[cb]
                if smp:
                    fw.dma(s_kv[oi], foa, [fo], [], fo, is_output=True)
                else:
                    fw.dma(o_kv[oi][g * 128:(g + 1) * 128, :], foa, [fo], [], fo, is_output=True)
                tb = kb if cb in (1, 4) else vb
                c0 = 0 if cb in (1, 2) else 512
                if cb in (1, 4):
                    fw.op(ACT, [fo], [tb], "copy", tb.t[:, c0:c0 + 512], foa)
                else:
                    fw.op(POOL, [fo], [tb], "tensor_copy", tb.t[:, c0:c0 + 512], foa)

        def trq(g_):
            transpose8(qb, PS[6], qT)
            fw.dma(QT.rearrange("c p t -> p c t")[:, :, g_ * 128:(g_ + 1) * 128], qT.t.rearrange("p (c t) -> p c t", t=128), [qT], [fw.reg("QT")], qT)

        def trk(g_):
            transpose8(kb, PS[7], kT)
            if g_ == NG:
                for be in range(2):
                    fw.dma(KTS[be].rearrange("c p t -> p c t")[:, :, PAST:PAST + 64], kT.t.rearrange("p (c t) -> p c t", t=128)[:, :, be * 64:(be + 1) * 64], [kT], [fw.reg(("KTS", be))], kT)
            else:
                fw.dma(KT.rearrange("c p t -> p c t")[:, :, g_ * 128:(g_ + 1) * 128], kT.t.rearrange("p (c t) -> p c t", t=128), [kT], [fw.reg("KT")], kT)

        blk(0)
        if g > 0:
            trk(g - 1)
        blk(3)
        blk(1)
        blk(4)
        trq(g)
        blk(2)
        blk(5)
        if smp:
            for be in range(2):
                fw.dma(VSS[be, PAST:PAST + 64, :], vb.t[be * 64:(be + 1) * 64, :], [vb], [fw.reg(("VSS", be))], vb)
        else:
            fw.dma(VS[g * 128:(g + 1) * 128, :], vb.t, [vb], [fw.reg("VS")], vb)
        if g == NG:
            trk(g)

    b2 = [(be, kbk) for be in range(2) for kbk in range(PAST // 128)]

    def ldB2(it):
        be, kbk = b2[it]
        rows = slice(kbk * 128, (kbk + 1) * 128)
        ck = F4[it % 2]; cv = F4[2 + it % 2]
        for i in range(2):
            fw.dma(ck.t[:, i * 512:(i + 1) * 512], c_k[i][be, rows, :], [], [ck], ck)
            fw.dma(cv.t[:, i * 512:(i + 1) * 512], c_v[i][be, rows, :], [], [cv], cv)
    ldB2(0)
    for it in range(len(b2)):
        if it + 1 < len(b2):
            ldB2(it + 1)
        be, kbk = b2[it]
        rows = slice(kbk * 128, (kbk + 1) * 128)
        ck = F4[it % 2]; cv = F4[2 + it % 2]
        kbf = H2[it % 2]; vbf = H2[2 + it % 2]
        fw.op(DVE, [ck], [kbf], "tensor_copy", kbf.t, ck.t)
        fw.op(POOL, [cv], [vbf], "tensor_copy", vbf.t, cv.t)
        fw.dma(VSS[be, rows, :], vbf.t, [vbf], [fw.reg(("VSS", be))], vbf)
        kT = H2[4 + it % 2]
        transpose8(kbf, PS[it % 2], kT)
        fw.dma(KTS[be].rearrange("c p t -> p c t")[:, :, rows], kT.t.rearrange("p (c t) -> p c t", t=128), [kT], [fw.reg(("KTS", be))], kT)

    WdF = Wd.t.rearrange("p a b -> p (a b)")
    WgF = Wg.t.rearrange("p a b -> p (a b)")
    WuF = Wu.t.rearrange("p a b -> p (a b)")
    cur = {}

    def load_kv(Kb, Ksb, Vb, Vsb_, kt_src, kt_reg, v_src, v_reg, nk, vcol):
        n = nk * 128
        for c0 in range(0, n, 2048):
            c1 = min(n, c0 + 2048)
            fw.dma(Ksb[:, c0:c1], kt_src[:, c0:c1], [kt_reg], [Kb], Kb, chain=True, q=POOL)
        vv = v_src.rearrange("(n p) c -> p n c", p=128)
        vs = Vsb_[:, 0:nk * 128].rearrange("p (n c) -> p n c", c=128)
        for k0 in range(0, nk, 8):
            k1 = min(nk, k0 + 8)
            fw.dma(vs[:, k0:k1, :], vv[:, k0:k1, vcol:vcol + 128], [v_reg], [Vb], Vb, chain=True, q=POOL)

    def kvset(kind, be=0):
        if kind == "p":
            return Wg, WgF[:, 0:T], Wu, WuF[:, 0:T]
        return Wd, WdF[:, be * TKS:(be + 1) * TKS], Wd, WdF[:, (2 + be) * TKS:(3 + be) * TKS]

    def ps2(b0):
        return PSALL[:, b0 * 512:(b0 + 2) * 512].rearrange("p (h n) -> p h n", h=2)

    def v2(buf):
        return buf.t.rearrange("p (h n) -> p h n", h=2)

    def load_q(qb, qz_ap, qsrc, qreg, N):
        fw.dma(qz_ap[0:64, 0:N], qsrc[0:64, :], [qreg], [qb], qb)
        fw.dma(qz_ap[64:128, 512:512 + N], qsrc[64:128, :], [qreg], [qb], qb, chain=True)

    QB = []
    for k in range(2):
        v = F4[2 + k].t.bitcast(BF16)
        QB.append((F4[2 + k], v[:, 0:1024], v[:, 1024:2048]))

    def load_q_sb(k, qsrc, qreg, N):
        qb, qz_ap, nq_ap = QB[k]
        load_q(qb, qz_ap, qsrc, qreg, N)
        fw.op(DVE, [qb], [qb], "tensor_scalar", nq_ap, qz_ap, -1.0, None, op0=ALU.mult)

    def sb_slot(qk, N, units, odst, oreg):
        Wg, KTsb, Wu, Vsb = cur["kv"]
        qz, qz_ap, nq_ap = QB[qk]
        nq = qz
        R = H2[4]
        ob = H2[7]
        nu = len(units)
        accs = (PS[6], PS[7])
        tpb = (PS[4], PS[5]); tp2 = ps2(4)
        fw.op(DVE, [], [R], "memset", R.t, 0.0)

        def bufs(ui):
            p = ui % 2
            return (PS[2 * p], PS[2 * p + 1]), ps2(2 * p), F4[p], H2[p], H2[2 + p]

        def stA(ui):
            kb, kn, mbuf, map_, bias = units[ui]
            zb_, z2, e, sp, w = bufs(ui)
            for hh in range(2):
                fw.op(PE, [Wg, qz], [zb_[hh]], "matmul", z2[0:kn, hh, 0:N], KTsb[:, kb * 128:kb * 128 + kn], qz_ap[:, hh * 512:hh * 512 + N], start=True, stop=True)

        def stB(ui):
            kb, kn, mbuf, map_, bias = units[ui]
            zb_, z2, e, sp, w = bufs(ui)
            bb, bap = bias
            fw.op(ACT, [zb_[0], zb_[1], bb], [e], "activation", v2(e)[0:kn, :, 0:N], z2[0:kn, :, 0:N], AF.Exp, bias=bap[0:kn, :], scale=1.0)

        def stB2(ui):
            kb, kn, mbuf, map_, bias = units[ui]
            zb_, z2, e, sp, w = bufs(ui)
            fw.op(ACT, [e], [sp], "activation", v2(sp)[0:kn, :, 0:N], v2(e)[0:kn, :, 0:N], AF.Ln, bias=1.0, scale=1.0)
            if mbuf is not None:
                for hh in range(2):
                    fw.op(DVE, [sp, mbuf], [sp], "tensor_tensor", sp.t[0:kn, hh * 512:hh * 512 + N], sp.t[0:kn, hh * 512:hh * 512 + N], map_, op=ALU.mult)

        def stC(ui):
            kb, kn, mbuf, map_, bias = units[ui]
            zb_, z2, e, sp, w = bufs(ui)
            first = ui == 0
            ks = slice(kb * 128, kb * 128 + kn)
            for hh in range(2):
                cs = slice(hh * 512, hh * 512 + N)
                fw.op(PE, [Wg, nq], [tpb[hh]], "matmul", tp2[0:kn, hh, 0:N], KTsb[:, ks], nq_ap[:, cs], start=True, stop=False, signal=False)
                fw.op(PE, [uinc, sp], [tpb[hh]], "matmul", tp2[0:kn, hh, 0:N], uinc.t[0:kn, 0:kn], sp.t[0:kn, cs], start=False, stop=first, signal=first)
                if not first:
                    fw.op(PE, [ones, R], [tpb[hh]], "matmul", tp2[0:kn, hh, 0:N], ones.t[:, 0:kn], R.t[:, cs], start=False, stop=True)
            if ui != nu - 1:
                fw.op(DVE, [R, sp], [R], "tensor_tensor", v2(R)[0:kn, :, 0:N], v2(R)[0:kn, :, 0:N], v2(sp)[0:kn, :, 0:N], op=ALU.add)

        def stD(ui):
            kb, kn, mbuf, map_, bias = units[ui]
            zb_, z2, e, sp, w = bufs(ui)
            bb, bap = bias
            fw.op(ACT, [tpb[0], tpb[1], bb], [w], "activation", v2(w)[0:kn, :, 0:N], tp2[0:kn, :, 0:N], AF.Exp, bias=bap[0:kn, :], scale=-1.0)
            if mbuf is not None:
                for hh in range(2):
                    fw.op(DVE, [w, mbuf], [w], "tensor_tensor", w.t[0:kn, hh * 512:hh * 512 + N], w.t[0:kn, hh * 512:hh * 512 + N], map_, op=ALU.mult)

        def stE(ui):
            kb, kn, mbuf, map_, bias = units[ui]
            zb_, z2, e, sp, w = bufs(ui)
            for hh in range(2):
                fw.op(PE, [Wu, w], [accs[hh]], "matmul", accs[hh].t[:, 0:N], Vsb[0:kn, kb * 128:(kb + 1) * 128], w.t[0:kn, hh * 512:hh * 512 + N], start=(ui == 0), stop=(ui == nu - 1), signal=(ui == nu - 1))

        for s_ in range(-2, nu):
            if 0 <= s_ + 2 < nu:
                stA(s_ + 2)
            if 0 <= s_ + 1 < nu:
                stB(s_ + 1)
                stB2(s_ + 1)
            if 0 <= s_ < nu:
                stD(s_)
            if 0 <= s_ + 1 < nu:
                stC(s_ + 1)
            if 0 <= s_ < nu:
                stE(s_)
        for hh in range(2):
            pr = slice(64 * hh, 64 * hh + 64)
            fw.op(DVE, [accs[hh]], [ob], "tensor_copy", ob.t[pr, 0:N], accs[hh].t[pr, 0:N])
        fw.dma(odst, ob.t[:, 0:N].rearrange("p (g t) -> p g t", t=128) if N == 512 else ob.t[:, 0:N], [ob], [oreg], ob)

    def df_slot(qk, N, units, odst, oreg):
        Wg, KTsb, Wu, Vsb = cur["kv"]
        qz = H2[5 + qk]
        ob = H2[7]
        accs = (PS[4], PS[5])
        psacc = F4[4]
        nu = len(units)
        fw.op(DVE, [], [psacc], "memset", psacc.t, 0.0)

        def bufs(ui):
            p = ui % 3
            b0 = (0, 2, 6)[p]
            return (PS[b0], PS[b0 + 1]), ps2(b0), H2[p]

        def stA(ui):
            kb, kn, mbuf, map_, bias = units[ui]
            sb_, s2, p = bufs(ui)
            for m in range(2):
                fw.op(PE, [Wg, qz], [sb_[m]], "matmul", s2[0:kn, m, 0:N], KTsb[:, kb * 128:kb * 128 + kn], qz.t[:, m * 512:m * 512 + N], start=True, stop=True)

        def stB(ui):
            kb, kn, mbuf, map_, bias = units[ui]
            sb_, s2, p = bufs(ui)
            bb, bap = bias
            fw.op(ACT, [sb_[0], sb_[1], bb], [p], "activation", v2(p)[0:kn, :, 0:N], s2[0:kn, :, 0:N], AF.Exp, bias=bap[0:kn, :], scale=1.0)
            if mbuf is not None:
                for m in range(2):
                    fw.op(DVE, [p, mbuf], [p], "tensor_tensor", p.t[0:kn, m * 512:m * 512 + N], p.t[0:kn, m * 512:m * 512 + N], map_, op=ALU.mult)
            fw.op(DVE, [psacc, p], [psacc], "tensor_tensor", v2(psacc)[0:kn, :, 0:N], v2(psacc)[0:kn, :, 0:N], v2(p)[0:kn, :, 0:N], op=ALU.add)

        def stC(ui):
            kb, kn, mbuf, map_, bias = units[ui]
            sb_, s2, p = bufs(ui)
            first = ui == 0; last = ui == nu - 1
            for m in range(2):
                fw.op(PE, [Wu, p], [accs[m]], "matmul", accs[m].t[:, 0:N], Vsb[0:kn, kb * 128:(kb + 1) * 128], p.t[0:kn, m * 512:m * 512 + N], start=first, stop=last, signal=last)

        for s_ in range(-2, nu):
            if 0 <= s_ + 2 < nu:
                stA(s_ + 2)
            if 0 <= s_ < nu:
                stB(s_)
                stC(s_)
        sums = (PS[6], PS[7])
        for m in range(2):
            fw.op(PE, [ones32, psacc], [sums[m]], "matmul", sums[m].t[:, 0:N], ones32.t, psacc.t[:, m * 512:m * 512 + N], start=True, stop=True)
        r0 = F4[0]; r1 = F4[1]; a0 = F4[2]; o = F4[3]
        for sm_, r_ in ((sums[0], r0), (sums[1], r1)):
            fw.op(ACT, [sm_], [r_], "activation", r_.t[:, 0:N], sm_.t[:, 0:N], AF.Ln)
            fw.op(ACT, [r_], [r_], "activation", r_.t[:, 0:N], r_.t[:, 0:N], AF.Exp, scale=-1.0)
        fw.op(DVE, [accs[0], r0], [a0], "tensor_tensor", a0.t[:, 0:N], accs[0].t[:, 0:N], r0.t[:, 0:N], op=ALU.mult)
        fw.op(DVE, [accs[1], r1], [r1], "tensor_tensor", r1.t[:, 0:N], accs[1].t[:, 0:N], r1.t[:, 0:N], op=ALU.mult)
        fw.op(DVE, [r1, nlam, a0], [o], "scalar_tensor_tensor", o.t[:, 0:N], r1.t[:, 0:N], nlam.t, a0.t[:, 0:N], op0=ALU.mult, op1=ALU.add)
        sq = H2[3]
        fw.op(DVE, [o], [sq], "tensor_tensor", sq.t[:, 0:N], o.t[:, 0:N], o.t[:, 0:N], op=ALU.mult)
        ms = PS[0]
        fw.op(PE, [ones, sq], [ms], "matmul", ms.t[:, 0:N], ones.t, sq.t[:, 0:N], start=True, stop=True)
        fw.op(ACT, [ms, epsc], [r0], "activation", r0.t[:, 0:N], ms.t[:, 0:N], AF.Ln, bias=epsc.t, scale=1.0 / 128.0)
        fw.op(ACT, [r0], [r0], "activation", r0.t[:, 0:N], r0.t[:, 0:N], AF.Exp, scale=-0.5)
        fw.op(DVE, [o, gcol, r0], [ob], "scalar_tensor_tensor", ob.t[:, 0:N], o.t[:, 0:N], gcol.t, r0.t[:, 0:N], op0=ALU.mult, op1=ALU.mult)
        fw.dma(odst, ob.t[:, 0:N].rearrange("p (g t) -> p g t", t=128) if N == 512 else ob.t[:, 0:N], [ob], [oreg], ob)

    zb = (zcol, zcol.t)
    k0b = (kb0t, kb0t.t)

    def prompt_units(j, desc):
        nkb = 8 * (j + 1)
        order = range(nkb - 1, -1, -1) if desc else range(nkb)
        us = []
        for kb in order:
            r = kb - (8 * j + 4)
            if r >= 0:
                us.append((kb, 128, MSK, MSKv[:, r, :], zb))
            elif kb < 4:
                us.append((kb, 128, None, None, k0b))
            else:
                us.append((kb, 128, None, None, zb))
        return us

    def smp_units(desc, masked):
        us = []
        order = range(NKS - 1, -1, -1) if desc else range(NKS)
        for kb in order:
            if kb == NKS - 1:
                us.append((kb, 64, MSMP if masked else None, MSMP.t[0:64, :] if masked else None, zb))
            else:
                us.append((kb, 128, None, None, zb))
        return us

    fw.op(POOL, [], [H2[5]], "memset", H2[5].t, 0.0)
    fw.op(POOL, [], [H2[6]], "memset", H2[6].t, 0.0)
    fw.op(POOL, [], [F4[2]], "memset", F4[2].t, 0.0)
    fw.op(POOL, [], [F4[3]], "memset", F4[3].t, 0.0)
    segs = []
    for ch in range(8):
        segs.append((ch, "p"))
        segs.append((ch, "s"))

    def seg_load(i):
        ch, kind = segs[i]
        vcol = ch * 128
        if kind == "p":
            Kb, Ka, Vb, Va = kvset("p")
            load_kv(Kb, Ka, Vb, Va, KT[ch], fw.reg("KT"), VS, fw.reg("VS"), T // 128, vcol)
        else:
            for be in range(2):
                Kb, Ka, Vb, Va = kvset("s", be)
                load_kv(Kb, Ka, Vb, Va, KTS[be, ch], fw.reg(("KTS", be)), VSS[be], fw.reg(("VSS", be)), NKS, vcol)

    sbs = []
    for ch in range(4):
        for j in range(8):
            q0 = (2 * j + 1) * 512
            sbs.append((QT[ch, :, q0:q0 + 512], 512))
        for be in range(2):
            sbs.append((QT[ch, :, T + be * 64:T + be * 64 + 64], 64))
    sbi = {"i": 0}

    dfs = []
    for ch in range(4, 8):
        for j in range(8):
            q0 = (2 * j + 1) * 512
            dfs.append((QT[ch, :, q0:q0 + 512], 512))
        for be in range(2):
            dfs.append((QT[ch, :, T + be * 64:T + be * 64 + 64], 64))
    dfi = {"i": 0}

    def run_df(units, N, odst):
        i = dfi["i"]
        if i == 0:
            load_q(H2[5], H2[5].t, dfs[0][0], fw.reg("QT"), dfs[0][1])
        if i + 1 < len(dfs):
            qb = H2[5 + (i + 1) % 2]
            load_q(qb, qb.t, dfs[i + 1][0], fw.reg("QT"), dfs[i + 1][1])
        df_slot(i % 2, N, units, odst, fw.reg("OT"))
        dfi["i"] = i + 1

    def run_sb(units, N, odst):
        i = sbi["i"]
        if i + 1 < len(sbs):
            load_q_sb((i + 1) % 2, sbs[i + 1][0], fw.reg("QT"), sbs[i + 1][1])
        sb_slot(i % 2, N, units, odst, fw.reg("OT"))
        sbi["i"] = i + 1

    def seg_run(i):
        ch, kind = segs[i]
        issb = ch < 4
        if kind == "p":
            cur["kv"] = kvset("p")
            if ch in (0, 4):
                fw.dma(MSKv, msbd if issb else mdfd, [], [MSK], MSK)
            for j in range(8):
                q0 = (2 * j + 1) * 512
                od = OT[4 * j:4 * j + 4, :, ch, :].rearrange("g p t -> p g t")
                if issb:
                    run_sb(prompt_units(j, True), 512, od)
                else:
                    run_df(prompt_units(j, False), 512, od)
        else:
            for be in range(2):
                cur["kv"] = kvset("s", be)
                qs = QT[ch, :, T + be * 64:T + be * 64 + 64]
                od = OT[NG // 2, :, ch, be * 64:(be + 1) * 64]
                if issb:
                    run_sb(smp_units(True, True), 64, od)
                else:
                    run_df(smp_units(False, False), 64, od)

    load_q_sb(0, sbs[0][0], fw.reg("QT"), sbs[0][1])
    seg_load(0)
    for i in range(len(segs)):
        if i + 1 < len(segs):
            seg_load(i + 1)
        seg_run(i)

    for c in range(8):
        for q in range(2):
            load_cast(Wd, Wd.t[:, c, q * 512:(q + 1) * 512], wo[c * 128:(c + 1) * 128, q * 512:(q + 1) * 512], 512)
    load_ln(2)
    NGD = NG // 2 + 1

    def x1rows(g):
        if g < NG // 2:
            s = 2 * (g // 4) + 1
            r0 = s * 512 + (g % 4) * 128
            return X1[r0:r0 + 128, :], fw.reg(("X1", r0 // 128))
        return X1[T:T + 128, :], fw.reg(("X1", NG))

    LNS = []
    for k in range(3):
        LNS.append((sb("stats%d" % k, [128, 2, 6], F32), sb("mv%d" % k, [128, 2], F32), sb("lnv%d" % k, [128, 1], F32), sb("rstd%d" % k, [128, 1], F32)))

    def ldD(g):
        og = H2[g % 3]; rt = F4[g % 4]
        fw.dma(og.t, OT[g].rearrange("p c t -> p (c t)"), [fw.reg("OT")], [og], og)
        ap, rb = x1rows(g)
        fw.dma(rt.t, ap, [rb], [rt], rt)

    def D1a(g):
        og = H2[g % 3]; rt = F4[g % 4]
        st_, mv_, lnv_, rstd_ = LNS[g % 3]
        for half in range(2):
            for c in range(8):
                fw.op(PE, [og, Wd], [PS[half]], "matmul", PS[half].t, og.t[:, c * 128:(c + 1) * 128], Wd.t[:, c, half * 512:(half + 1) * 512], start=(c == 0), stop=(c == 7), signal=(c == 7))
        for half in range(2):
            fw.op(DVE, [PS[half], rt], [rt], "scalar_tensor_tensor", rt.t[:, half * 512:(half + 1) * 512], PS[half].t, 1.0 / ALPHA, rt.t[:, half * 512:(half + 1) * 512], op0=ALU.mult, op1=ALU.add)
        for h in range(2):
            fw.op(DVE, [rt], [st_], "bn_stats", st_.t[:, h, :], rt.t[:, h * 512:(h + 1) * 512])
        fw.op(DVE, [st_], [mv_], "bn_aggr", mv_.t, st_.t)

    def D1b(g):
        st_, mv_, lnv_, rstd_ = LNS[g % 3]
        fw.op(ACT, [mv_, epsl], [lnv_], "activation", lnv_.t, mv_.t[:, 1:2], AF.Ln, bias=epsl.t, scale=1.0)
        fw.op(ACT, [lnv_], [rstd_], "activation", rstd_.t, lnv_.t, AF.Exp, scale=-0.5)

    def D1c(g):
        rt = F4[g % 4]
        st_, mv_, lnv_, rstd_ = LNS[g % 3]
        fw.op(DVE, [rt, mv_, rstd_], [rt], "tensor_scalar", rt.t, rt.t, mv_.t[:, 0:1], rstd_.t, op0=ALU.subtract, op1=ALU.mult)
        fw.op(POOL, [rt, F4[4]], [rt], "tensor_tensor", rt.t, rt.t, F4[4].t, op=ALU.mult)
        fw.op(POOL, [rt, F4[5]], [rt], "tensor_tensor", rt.t, rt.t, F4[5].t, op=ALU.add)
        fw.dma(X2[g * 128:(g + 1) * 128, :], rt.t, [rt], [fw.reg(("X2", g))], rt)

    ldD(0)
    ldD(1)
    f2chunks = ffn_w_chunks(f2)
    f2i = {"i": 0}
    for g in range(NGD + 2):
        if g < NGD:
            D1a(g)
            for _ in range(2):
                if f2i["i"] < 64:
                    load_cast(*f2chunks[f2i["i"]])
                    f2i["i"] += 1
        if 0 <= g - 1 < NGD:
            D1b(g - 1)
        if 0 <= g - 2 < NGD:
            D1c(g - 2)
        if g + 2 < NGD:
            ldD(g + 2)

    def srcD(g):
        return X2[g * 128:(g + 1) * 128, :], [fw.reg(("X2", g))]

    def dstD(g, rt):
        if g < NG // 2:
            fw.dma(yp[g * 128:(g + 1) * 128, :], rt.t, [rt], [], rt, is_output=True)
        else:
            fw.dma(ys, rt.t, [rt], [], rt, is_output=True)

    ffn_stage(NGD, srcD, f2, 4, dstD, False, skip=f2i["i"])
    fw.finish()
    return nc


def _consts():
    bf = ml_dtypes.bfloat16
    p = np.arange(128)[:, None]
    ident = (p == np.arange(128)[None, :]).astype(np.float32).astype(bf)
    uinc = (p >= np.arange(128)[None, :]).astype(np.float32).astype(bf)
    t = np.arange(512)[None, None, :]
    r = np.arange(4)[None, :, None]
    s = 128 * r + p[:, :, None]
    msb = (s < t).astype(np.float32).astype(bf)
    mdf = ((s // 64) <= (t // 64)).astype(np.float32).astype(bf)
    msmp = (p < np.arange(64)[None, :]).astype(np.float32).astype(bf)
    return ident, uinc, np.ascontiguousarray(msb), np.ascontiguousarray(mdf), msmp


def _rope_tab(pos):
    half = 32
    inv = (10000.0 ** (-np.arange(half, dtype=np.float32) / half)).astype(np.float32)
    ang = pos.astype(np.float32)[:, None] * inv[None, :]
    return np.cos(ang).astype(np.float32), np.sin(ang).astype(np.float32)


def kernel(**inp):
    f = lambda k: np.ascontiguousarray(np.asarray(inp[k], dtype=np.float32))
    x_prompt = f("x_prompt"); x_sample = f("x_sample")
    ident, uinc, msb, mdf, msmp = _consts()
    nc = build()
    coss, sins = _rope_tab(PAST + np.concatenate([np.arange(64), np.arange(64)]))
    shared = {
        "f1g": f("ffn1_wg")[0], "f1u": f("ffn1_wu")[0], "f1d": f("ffn1_wd")[0],
        "f2g": f("ffn2_wg")[0], "f2u": f("ffn2_wu")[0], "f2d": f("ffn2_wd")[0],
        "win": f("w_in")[0], "wo": f("w_o")[0],
        "ln1g": f("ln1_g"), "ln1b": f("ln1_b"), "ln2g": f("ln2_g"), "ln2b": f("ln2_b"), "ln3g": f("ln3_g"), "ln3b": f("ln3_b"),
        "lq1": f("lambda_q1"), "lk1": f("lambda_k1"), "lq2": f("lambda_q2"), "lk2": f("lambda_k2"),
        "subg": np.ascontiguousarray(f("subln_g").reshape(128, 1)),
        "coss": coss, "sins": sins, "ident": ident, "uinc": uinc, "msb": msb, "mdf": mdf, "msmp": msmp,
    }
    csk = f("cache_sb_k")[0].reshape(16, PAST, 512); csv = f("cache_sb_v")[0].reshape(16, PAST, 512)
    cdk = f("cache_diff_k")[0].reshape(16, PAST, 512); cdv = f("cache_diff_v")[0].reshape(16, PAST, 512)
    in_maps = []
    for c in range(8):
        b, h = c // 2, c % 2
        if h == 0:
            xpc = np.concatenate([np.zeros((512, D), np.float32), x_prompt[b, :T - 512]], axis=0)
            pos = np.arange(T) - 512
        else:
            xpc = x_prompt[b]
            pos = np.arange(T)
        cp, sp = _rope_tab(np.maximum(pos, 0))
        m = dict(shared)
        m.update({
            "xp": np.ascontiguousarray(xpc), "xs": np.ascontiguousarray(x_sample[2 * c:2 * c + 2].reshape(128, D)),
            "c_sbk": np.ascontiguousarray(csk[2 * c:2 * c + 2]), "c_sbv": np.ascontiguousarray(csv[2 * c:2 * c + 2]),
            "c_dfk": np.ascontiguousarray(cdk[2 * c:2 * c + 2]), "c_dfv": np.ascontiguousarray(cdv[2 * c:2 * c + 2]),
            "cosp": cp, "sinp": sp,
            "kb0": np.full((128, 1), NEGBIG if h == 0 else 0.0, np.float32),
        })
        in_maps.append(m)
    res = run_bass_kernel_spmd(nc, in_maps, core_ids=list(range(8)))
    R = res.results
    y_p = np.zeros((4, T, D), np.float32)
    y_s = np.zeros((16, 64, D), np.float32)
    kvp = [np.zeros((1, 4, T, 512), np.float32) for _ in range(4)]
    kvs = [np.zeros((1, 16, 64, 512), np.float32) for _ in range(4)]
    names_p = ("o_sbk", "o_sbv", "o_dfk", "o_dfv"); names_s = ("s_sbk", "s_sbv", "s_dfk", "s_dfv")
    for c in range(8):
        b, h = c // 2, c % 2
        yo = R[c]["yp"].reshape(8, 512, D)
        for j in range(8):
            sblk = 2 * j + h
            y_p[b, sblk * 512:(sblk + 1) * 512] = yo[j]
        y_s[2 * c:2 * c + 2] = R[c]["ys"].reshape(2, 64, D)
        for i in range(4):
            kvs[i][0, 2 * c:2 * c + 2] = R[c][names_s[i]].reshape(2, 64, 512)
            if h == 1:
                kvp[i][0, b] = R[c][names_p[i]]
    return (y_p, y_s,
            kvp[0].reshape(1, 4, T, 8, 64), kvp[1].reshape(1, 4, T, 8, 64),
            kvp[2].reshape(1, 4, T, 4, 2, 64), kvp[3].reshape(1, 4, T, 4, 128),
            kvs[0].reshape(1, 16, 64, 8, 64), kvs[1].reshape(1, 16, 64, 8, 64),
            kvs[2].reshape(1, 16, 64, 4, 2, 64), kvs[3].reshape(1, 16, 64, 4, 128))
```

```python
import math
import numpy as np
import ml_dtypes
import concourse.bass as bass
import concourse.mybir as mybir
from concourse.bass_utils import run_bass_kernel_spmd

F32 = mybir.dt.float32
BF16 = mybir.dt.bfloat16
AF = mybir.ActivationFunctionType
ALU = mybir.AluOpType
AX = mybir.AxisListType

D = 1024
DFF = 2816
NFC = 22
T = 8192
NG = 64
PAST = 4096
NKS = 33
TKS = 4224
ALPHA = 2.0 ** 0.25
EPS = 1e-5
LAM0 = 0.8 - 0.6 * math.exp(0.0)
NEGBIG = -30000.0
SAME_SYNC = True


class Buf:
    def __init__(self, t=None, name=""):
        self.t = t
        self.name = name
        self.lw = {}
        self.rd = {}
        self.dsem = None
        self.dn = 0


class Eng:
    def __init__(self, fw, name, is_pe=False):
        self.name = name
        self.is_pe = is_pe
        self.sem = fw.nc.alloc_semaphore("sem_" + name)
        self.n = 0
        self.waited = {}
        self.prog = []

    def wait(self, ev):
        sem, val = ev
        if sem is self.sem:
            if self.is_pe or not SAME_SYNC:
                return
            assert val <= self.n
        k = id(sem)
        if self.waited.get(k, 0) >= val:
            return
        self.prog.append(("wait", sem, val))
        self.waited[k] = val


class FW:
    def __init__(self, nc):
        self.nc = nc
        self.pe = Eng(self, "pe", True)
        self.act = Eng(self, "act")
        self.dve = Eng(self, "dve")
        self.pool = Eng(self, "pool")
        self.sp = Eng(self, "sp")
        self.out_events = []
        self.regs = {}
        self.ndsem = 0

    def sb(self, name, shape, dtype):
        return Buf(self.nc.alloc_sbuf_tensor(name, list(shape), dtype).ap(), name)

    def ps(self, name, shape, dtype=F32):
        return Buf(self.nc.alloc_psum_tensor(name, list(shape), dtype).ap(), name)

    def reg(self, key):
        b = self.regs.get(key)
        if b is None:
            b = Buf(None, str(key))
            self.regs[key] = b
        return b

    def _deps(self, E, reads, writes):
        for b in reads:
            for ev in b.lw.values():
                E.wait(ev)
        for b in writes:
            for ev in b.lw.values():
                E.wait(ev)
            for ev in b.rd.values():
                E.wait(ev)

    def _record(self, ev, reads, writes):
        k = id(ev[0])
        for b in writes:
            if b.t is None:
                b.lw[k] = ev
            else:
                b.lw = {k: ev}
                b.rd = {}
        for b in reads:
            if b in writes:
                continue
            old = b.rd.get(k)
            if old is None or old[1] < ev[1]:
                b.rd[k] = ev

    def op(self, E, reads, writes, meth, *args, signal=True, **kw):
        self._deps(E, reads, writes)
        if signal:
            E.prog.append(("op", meth, args, kw, E.sem))
            E.n += 1
            ev = (E.sem, E.n)
        else:
            E.prog.append(("op", meth, args, kw, None))
            ev = (E.sem, E.n + 1)
        self._record(ev, reads, writes)

    def dma(self, out_ap, in_ap, reads, writes, owner, is_output=False, chain=False, q=None):
        Q = self.sp if q is None else q
        if owner.dsem is None:
            owner.dsem = self.nc.alloc_semaphore("d%d" % self.ndsem)
            self.ndsem += 1
        self._deps(Q, reads, writes)
        if owner.dn > 0 and not chain:
            Q.wait((owner.dsem, 16 * owner.dn))
        Q.prog.append(("dma", out_ap, in_ap, owner.dsem))
        owner.dn += 1
        ev = (owner.dsem, 16 * owner.dn)
        self._record(ev, reads, writes)
        if is_output:
            self.out_events.append(ev)

    def finish(self):
        last = {}
        for ev in self.out_events:
            k = id(ev[0])
            if k not in last or last[k][1] < ev[1]:
                last[k] = ev
        for ev in last.values():
            self.sp.wait(ev)
        nc = self.nc
        with nc.Block() as block:
            for E, dec in ((self.sp, block.sync), (self.pe, block.tensor), (self.act, block.scalar),
                           (self.dve, block.vector), (self.pool, block.gpsimd)):
                def body(e, E=E):
                    for it in E.prog:
                        if it[0] == "wait":
                            e.wait_ge(it[1], it[2])
                        elif it[0] == "dma":
                            e.dma_start(out=it[1], in_=it[2]).then_inc(it[3], 16)
                        else:
                            ins = getattr(e, it[1])(*it[2], **it[3])
                            if it[4] is not None:
                                ins.then_inc(it[4], 1)
                dec(body)


def build():
    nc = bass.Bass("TRN2", target_bir_lowering=False)

    def din(name, shape, dt=F32):
        return nc.dram_tensor(name, list(shape), dt, kind="ExternalInput").ap()

    def dout(name, shape):
        return nc.dram_tensor(name, list(shape), F32, kind="ExternalOutput").ap()

    def dscr(name, shape, dt):
        return nc.dram_tensor(name, list(shape), dt, kind="Internal").ap()

    xp = din("xp", [T, D]); xs = din("xs", [128, D])
    c_k = [din("c_sbk", [2, PAST, 512]), din("c_dfk", [2, PAST, 512])]
    c_v = [din("c_sbv", [2, PAST, 512]), din("c_dfv", [2, PAST, 512])]
    f1 = (din("f1g", [D, DFF]), din("f1u", [D, DFF]), din("f1d", [DFF, D]))
    f2 = (din("f2g", [D, DFF]), din("f2u", [D, DFF]), din("f2d", [DFF, D]))
    win = din("win", [D, 3 * D]); wo = din("wo", [D, D])
    lnp = [din(n, [1, D]) for n in ("ln1g", "ln1b", "ln2g", "ln2b", "ln3g", "ln3b")]
    lamv = [din(n, [1, 64]) for n in ("lq1", "lk1", "lq2", "lk2")]
    subg = din("subg", [128, 1])
    cosp = din("cosp", [T, 32]); sinp = din("sinp", [T, 32])
    coss = din("coss", [128, 32]); sins = din("sins", [128, 32])
    kb0 = din("kb0", [128, 1])
    identd = din("ident", [128, 128], BF16); uincd = din("uinc", [128, 128], BF16)
    msbd = din("msb", [128, 4, 512], BF16); mdfd = din("mdf", [128, 4, 512], BF16)
    msmpd = din("msmp", [128, 64], BF16)

    yp = dout("yp", [T // 2, D]); ys = dout("ys", [128, D])
    o_kv = [dout(n, [T, 512]) for n in ("o_sbk", "o_sbv", "o_dfk", "o_dfv")]
    s_kv = [dout(n, [128, 512]) for n in ("s_sbk", "s_sbv", "s_dfk", "s_dfv")]

    X1 = dscr("X1", [T + 128, D], F32)
    X1T = dscr("X1T", [NG + 1, 128, D], BF16)
    QT = dscr("QT", [8, 128, T + 128], BF16)
    KT = dscr("KT", [8, 128, T], BF16)
    KTS = dscr("KTS", [2, 8, 128, TKS], BF16)
    VS = dscr("VS", [T, D], BF16)
    VSS = dscr("VSS", [2, TKS, D], BF16)
    OT = dscr("OT", [NG // 2 + 1, 128, 8, 128], BF16)
    X2 = dscr("X2", [T // 2 + 128, D], F32)

    fw = FW(nc)
    PE, ACT, DVE, POOL = fw.pe, fw.act, fw.dve, fw.pool
    sb = fw.sb
    Wg = sb("Wg", [128, 8, DFF], BF16); Wu = sb("Wu", [128, 8, DFF], BF16); Wd = sb("Wd", [128, NFC, D], BF16)
    F4 = [sb("F4_%d" % i, [128, D], F32) for i in range(6)]
    H2 = [sb("H2_%d" % i, [128, D], BF16) for i in range(8)]
    ST = [sb("ST_%d" % i, [128, 704], F32) for i in range(2)]
    HTB = sb("HTB", [128, DFF], BF16)
    MSK = F4[5]
    MSKv = F4[5].t.bitcast(BF16).rearrange("p (r n) -> p r n", n=512)
    MSMP = sb("MSMP", [128, 64], BF16)
    ident = sb("identb", [128, 128], BF16); uinc = sb("uincb", [128, 128], BF16); ones = sb("onesb", [128, 128], BF16)
    kb0t = sb("kb0t", [128, 1], F32); zcol = sb("zcol", [128, 1], F32); epsc = sb("epsc", [128, 1], F32)
    gcol = sb("gcol", [128, 1], F32); nlam = sb("nlam", [128, 1], F32)
    lamt = sb("lamt", [128, 4, 64], F32); lamp = sb("lamp", [128, 2, 64], F32); lams = sb("lams", [128, 2], F32)
    stats = sb("stats", [128, 2, 6], F32); mv = sb("mv", [128, 2], F32); lnv = sb("lnv", [128, 1], F32); rstd = sb("rstd", [128, 1], F32)
    cst = sb("cst", [128, 2, 32], F32)
    RT = [sb("RT_%d" % i, [128, 8, 32], F32) for i in range(2)]
    PSALL = nc.alloc_psum_tensor("PSALL", [128, 4096], F32).ap()
    PS = [Buf(PSALL[:, i * 512:(i + 1) * 512], "PS%d" % i) for i in range(8)]
    ones32 = sb("ones32", [128, 128], F32)

    def psbf(b):
        return b.t.bitcast(BF16)

    def pbc(ap1):
        return ap1.partition_broadcast(128)

    fw.dma(ident.t, identd, [], [ident], ident)
    fw.dma(uinc.t, uincd, [], [uinc], uinc)
    fw.dma(kb0t.t, kb0, [], [kb0t], kb0t)
    fw.dma(gcol.t, subg, [], [gcol], gcol)
    fw.dma(MSMP.t, msmpd, [], [MSMP], MSMP)
    for i in range(4):
        fw.dma(lamt.t[:, i:i + 1, :], pbc(lamv[i]), [], [lamt], lamt)
    fw.op(POOL, [], [ones], "memset", ones.t, 1.0)
    fw.op(POOL, [], [ones32], "memset", ones32.t, 1.0)
    fw.op(POOL, [], [zcol], "memset", zcol.t, 0.0)
    fw.op(POOL, [], [epsc], "memset", epsc.t, EPS)
    epsl = sb("epsl", [128, 1], F32)
    fw.op(POOL, [], [epsl], "memset", epsl.t, EPS / (ALPHA * ALPHA))
    fw.op(DVE, [gcol], [gcol], "tensor_scalar", gcol.t, gcol.t, 1.0 - LAM0, None, op0=ALU.mult)
    fw.op(DVE, [lamt], [lamp], "tensor_tensor", lamp.t[:, 0, :], lamt.t[:, 0, :], lamt.t[:, 1, :], op=ALU.mult)
    fw.op(DVE, [lamt], [lamp], "tensor_tensor", lamp.t[:, 1, :], lamt.t[:, 2, :], lamt.t[:, 3, :], op=ALU.mult)
    fw.op(DVE, [lamp], [lams], "reduce_sum", lams.t, lamp.t, axis=AX.X)
    fw.op(ACT, [lams], [lams], "activation", lams.t, lams.t, AF.Exp)
    fw.op(DVE, [lams], [nlam], "tensor_tensor", nlam.t, lams.t[:, 1:2], lams.t[:, 0:1], op=ALU.subtract)
    fw.op(DVE, [nlam], [nlam], "tensor_scalar", nlam.t, nlam.t, -LAM0, None, op0=ALU.add)

    cnt = {"st": 0}

    def load_cast(dstb, dst_ap, src_ap, n):
        k = cnt["st"]; cnt["st"] += 1
        st = ST[k % 2]
        fw.dma(st.t[:, 0:n], src_ap, [], [st], st)
        if k % 2 == 0:
            fw.op(DVE, [st], [dstb], "tensor_copy", dst_ap, st.t[:, 0:n])
        else:
            fw.op(ACT, [st], [dstb], "copy", dst_ap, st.t[:, 0:n])

    def ffn_w_chunks(ws):
        ch = []
        for dc in range(8):
            for q in range(4):
                ch.append((Wg, Wg.t[:, dc, q * 704:(q + 1) * 704], ws[0][dc * 128:(dc + 1) * 128, q * 704:(q + 1) * 704], 704))
        for dc in range(8):
            for q in range(4):
                ch.append((Wu, Wu.t[:, dc, q * 704:(q + 1) * 704], ws[1][dc * 128:(dc + 1) * 128, q * 704:(q + 1) * 704], 704))
        for fc in range(NFC):
            for q in range(2):
                ch.append((Wd, Wd.t[:, fc, q * 512:(q + 1) * 512], ws[2][fc * 128:(fc + 1) * 128, q * 512:(q + 1) * 512], 512))
        return ch

    def load_ffn_w(ws, skip=0):
        for c in ffn_w_chunks(ws)[skip:]:
            load_cast(*c)

    def ln_stats(rt):
        for h in range(2):
            fw.op(DVE, [rt], [stats], "bn_stats", stats.t[:, h, :], rt.t[:, h * 512:(h + 1) * 512])
        fw.op(DVE, [stats], [mv], "bn_aggr", mv.t, stats.t)

    def ln_act():
        fw.op(ACT, [mv, epsl], [lnv], "activation", lnv.t, mv.t[:, 1:2], AF.Ln, bias=epsl.t, scale=1.0)
        fw.op(ACT, [lnv], [rstd], "activation", rstd.t, lnv.t, AF.Exp, scale=-0.5)

    def ln_apply(rt):
        fw.op(DVE, [rt, mv, rstd], [rt], "tensor_scalar", rt.t, rt.t, mv.t[:, 0:1], rstd.t, op0=ALU.subtract, op1=ALU.mult)
        fw.op(POOL, [rt, F4[4]], [rt], "tensor_tensor", rt.t, rt.t, F4[4].t, op=ALU.mult)
        fw.op(POOL, [rt, F4[5]], [rt], "tensor_tensor", rt.t, rt.t, F4[5].t, op=ALU.add)

    def layernorm(rt):
        ln_stats(rt)
        ln_act()
        ln_apply(rt)

    def load_ln(gi):
        fw.dma(F4[4].t.unsqueeze(1), pbc(lnp[gi]), [], [F4[4]], F4[4])
        fw.dma(F4[5].t.unsqueeze(1), pbc(lnp[gi + 1]), [], [F4[5]], F4[5])

    def transpose8(src, psb, dst, dst_ap=None):
        pb = psbf(psb)
        for c in range(8):
            fw.op(PE, [src, ident], [psb], "transpose", pb[:, c * 128:(c + 1) * 128], src.t[:, c * 128:(c + 1) * 128], ident.t, signal=(c == 7))
        fw.op(ACT, [psb], [dst], "copy", dst.t if dst_ap is None else dst_ap, pb)

    def ffn_stage(ngroups, src_fn, ws, lni, dst_fn, x1t, skip=0):
        load_ffn_w(ws, skip)
        load_ln(lni)
        nfb = (DFF + 511) // 512
        HB = []
        for fb_ in range(nfb):
            par = H2[5 + fb_ // 2]
            hbuf = Buf(par.t[:, (fb_ % 2) * 512:(fb_ % 2) * 512 + min(512, DFF - fb_ * 512)], "HB%d" % fb_)
            hbuf.lw = dict(par.lw); hbuf.rd = dict(par.rd)
            HB.append(hbuf)

        def ld(g):
            xt = F4[g % 2]
            ap, rb = src_fn(g)
            fw.dma(xt.t, ap, rb, [xt], xt)

        def PRE_cast(g):
            xt = F4[g % 2]
            xbf = H2[0]
            fw.op(ACT, [xt], [xbf], "copy", xbf.t, xt.t)
            if g + 1 < ngroups:
                ld(g + 1)

        def PRE_tr(g):
            transpose8(H2[0], PS[7], H2[1 + g % 2])

        def PRE(g):
            PRE_cast(g)
            PRE_tr(g)

        def ldr(g):
            rt = F4[2 + g % 2]
            ap, rb = src_fn(g)
            fw.dma(rt.t, ap, rb, [rt], rt)

        def GU(g, fb):
            xT = H2[1 + g % 2]
            f0 = fb * 512
            fn = min(512, DFF - f0)
            pg = PS[fb % 2]; pu = PS[2 + fb % 2]
            for dc in range(8):
                fw.op(PE, [Wg, xT], [pg], "matmul", pg.t[:, 0:fn], xT.t[:, dc * 128:(dc + 1) * 128], Wg.t[:, dc, f0:f0 + fn], start=(dc == 0), stop=(dc == 7), signal=(dc == 7))
            for dc in range(8):
                fw.op(PE, [Wu, xT], [pu], "matmul", pu.t[:, 0:fn], xT.t[:, dc * 128:(dc + 1) * 128], Wu.t[:, dc, f0:f0 + fn], start=(dc == 0), stop=(dc == 7), signal=(dc == 7))
            sg = ST[fb % 2]
            hb = HB[fb]
            fw.op(ACT, [pg], [sg], "activation", sg.t[:, 0:fn], pg.t[:, 0:fn], AF.Silu)
            fw.op(DVE, [sg, pu], [hb], "tensor_tensor", hb.t[:, 0:fn], sg.t[:, 0:fn], pu.t[:, 0:fn], op=ALU.mult)

        def TR(g, fb):
            f0 = fb * 512
            fn = min(512, DFF - f0)
            hb = HB[fb]
            hc = 0
            pb = PS[6 + fb % 2]
            pbv = psbf(pb)
            nch = fn // 128
            for c in range(nch):
                fw.op(PE, [hb, ident], [pb], "transpose", pbv[:, c * 128:(c + 1) * 128], hb.t[:, hc + c * 128:hc + (c + 1) * 128], ident.t, signal=(c == nch - 1))
            fw.op(ACT, [pb], [HTB], "copy", HTB.t[:, f0:f0 + fn], pbv[:, 0:fn])

        def DOWN(g):
            for half in range(2):
                for fc in range(NFC):
                    fw.op(PE, [HTB, Wd], [PS[4 + half]], "matmul", PS[4 + half].t, HTB.t[:, fc * 128:(fc + 1) * 128], Wd.t[:, fc, half * 512:(half + 1) * 512], start=(fc == 0), stop=(fc == NFC - 1), signal=(fc == NFC - 1))

        def POST_a(g):
            rt = F4[2 + g % 2]
            for half in range(2):
                fw.op(DVE, [PS[4 + half], rt], [rt], "scalar_tensor_tensor", rt.t[:, half * 512:(half + 1) * 512], PS[4 + half].t, 0.5 / ALPHA, rt.t[:, half * 512:(half + 1) * 512], op0=ALU.mult, op1=ALU.add)
            ln_stats(rt)

        def POST_b(g):
            rt = F4[2 + g % 2]
            ln_apply(rt)
            dst_fn(g, rt)
            if x1t:
                x1bf = H2[3]
                fw.op(POOL, [rt], [x1bf], "tensor_copy", x1bf.t, rt.t)

        def X1TR(g):
            x1bf = H2[3]
            x1T = H2[4]
            transpose8(x1bf, PS[7], x1T)
            r = fw.reg(("X1T", g))
            fw.dma(X1T[g], x1T.t, [x1T], [r], x1T)

        ld(0)
        PRE(0)
        for g in range(ngroups):
            for fb in range(nfb):
                GU(g, fb)
                if fb >= 1:
                    TR(g, fb - 1)
                if fb == 0:
                    if g > 0:
                        TR(g - 1, nfb - 1)
                        DOWN(g - 1)
                    ldr(g)
                if fb == 1:
                    if g > 0:
                        POST_a(g - 1)
                    if g + 1 < ngroups:
                        PRE_cast(g + 1)
                    if g > 1 and x1t:
                        X1TR(g - 2)
                if fb == 2:
                    if g + 1 < ngroups:
                        PRE_tr(g + 1)
                    if g > 0:
                        ln_act()
                if fb == 3 and g > 0:
                    POST_b(g - 1)
        TR(ngroups - 1, nfb - 1)
        DOWN(ngroups - 1)
        POST_a(ngroups - 1)
        ln_act()
        if x1t and ngroups > 1:
            X1TR(ngroups - 2)
        POST_b(ngroups - 1)
        for fb_ in range(nfb):
            par = H2[5 + fb_ // 2]
            for d_src, d_dst in ((HB[fb_].lw, par.lw), (HB[fb_].rd, par.rd)):
                for k, ev in d_src.items():
                    if k not in d_dst or d_dst[k][1] < ev[1]:
                        d_dst[k] = ev
        if x1t:
            X1TR(ngroups - 1)

    def srcA(g):
        if g < NG:
            return xp[g * 128:(g + 1) * 128, :], []
        return xs, []

    def dstA(g, rt):
        r = fw.reg(("X1", g))
        fw.dma(X1[g * 128:(g + 1) * 128, :], rt.t, [rt], [r], rt)

    ffn_stage(NG + 1, srcA, f1, 0, dstA, True)

    for dc in range(8):
        for cb in range(6):
            dstb = Wg if cb < 4 else Wu
            c0 = (cb % 4) * 512
            load_cast(dstb, dstb.t[:, dc, c0:c0 + 512], win[dc * 128:(dc + 1) * 128, cb * 512:(cb + 1) * 512], 512)

    def wview(dc, cb):
        b = Wg if cb < 4 else Wu
        c0 = (cb % 4) * 512
        return b, b.t[:, dc, c0:c0 + 512]

    def rope(psb, outb, out_ap):
        x = psb.t.rearrange("p (a b) -> p a b", b=64)
        o = out_ap.rearrange("p (a b) -> p a b", b=64)
        cb_ = cst.t[:, 0:1, :].to_broadcast([128, 8, 32]); sb_ = cst.t[:, 1:2, :].to_broadcast([128, 8, 32])
        tA, tB = RT
        fw.op(DVE, [psb, cst], [tA], "tensor_tensor", tA.t, x[:, :, 0:32], cb_, op=ALU.mult)
        fw.op(DVE, [psb, cst], [tB], "tensor_tensor", tB.t, x[:, :, 32:64], sb_, op=ALU.mult)
        fw.op(DVE, [tA, tB], [outb], "tensor_tensor", o[:, :, 0:32], tA.t, tB.t, op=ALU.subtract)
        fw.op(DVE, [psb, cst], [tA], "tensor_tensor", tA.t, x[:, :, 32:64], cb_, op=ALU.mult)
        fw.op(DVE, [psb, cst], [tB], "tensor_tensor", tB.t, x[:, :, 0:32], sb_, op=ALU.mult)
        fw.op(DVE, [tA, tB], [outb], "tensor_tensor", o[:, :, 32:64], tA.t, tB.t, op=ALU.add)

    bcnt = {"n": 0}

    def ldB(g):
        xt = H2[g % 2]
        fw.dma(xt.t, X1T[g], [fw.reg(("X1T", g))], [xt], xt)
    ldB(0)
    for g in range(NG + 1):
        if g + 1 < NG + 1:
            ldB(g + 1)
        x1T = H2[g % 2]
        smp = g == NG
        if smp:
            fw.dma(cst.t[:, 0, :], coss, [], [cst], cst); fw.dma(cst.t[:, 1, :], sins, [], [cst], cst)
        else:
            fw.dma(cst.t[:, 0, :], cosp[g * 128:(g + 1) * 128, :], [], [cst], cst)
            fw.dma(cst.t[:, 1, :], sinp[g * 128:(g + 1) * 128, :], [], [cst], cst)
        qb, kb, vb = H2[2], H2[3], H2[4]
        qT, kT = H2[5], H2[6]

        def blk(cb):
            k_ = bcnt["n"]; bcnt["n"] += 1
            pb_ = PS[k_ % 4]
            for dc in range(8):
                wb, wap = wview(dc, cb)
                fw.op(PE, [x1T, wb], [pb_], "matmul", pb_.t, x1T.t[:, dc * 128:(dc + 1) * 128], wap, start=(dc == 0), stop=(dc == 7), signal=(dc == 7))
            fo = F4[k_ % 4]
            foa = fo.t[:, 0:512]
            if cb == 0:
                fw.op(ACT, [pb_], [qb], "activation", qb.t[:, 0:512], pb_.t, AF.Identity, scale=0.125)
            elif cb == 3:
                rope(pb_, fo, foa)
                fw.op(ACT, [fo], [qb], "activation", qb.t[:, 512:1024], foa, AF.Identity, scale=0.125)
            else:
                if cb == 4:
                    rope(pb_, fo, foa)
                else:
                    fw.op(ACT, [pb_], [fo], "copy", foa, pb_.t)
                oi = {1: 0, 2: 1, 4: 2, 5: 3}[cb]
                if smp:
                    fw.dma(s_kv[oi], foa, [fo], [], fo, is_output=True)
                else:
                    fw.dma(o_kv[oi][g * 128:(g + 1) * 128, :], foa, [fo], [], fo, is_output=True)
                tb = kb if cb in (1, 4) else vb
                c0 = 0 if cb in (1, 2) else 512
                if cb in (1, 4):
                    fw.op(ACT, [fo], [tb], "copy", tb.t[:, c0:c0 + 512], foa)
                else:
                    fw.op(POOL, [fo], [tb], "tensor_copy", tb.t[:, c0:c0 + 512], foa)

        def trq(g_):
            transpose8(qb, PS[6], qT)
            fw.dma(QT.rearrange("c p t -> p c t")[:, :, g_ * 128:(g_ + 1) * 128], qT.t.rearrange("p (c t) -> p c t", t=128), [qT], [fw.reg("QT")], qT)

        def trk(g_):
            kT = H2[6 + g_ % 2]
            transpose8(kb, PS[7], kT)
            if g_ == NG:
                for be in range(2):
                    fw.dma(KTS[be].rearrange("c p t -> p c t")[:, :, PAST:PAST + 64], kT.t.rearrange("p (c t) -> p c t", t=128)[:, :, be * 64:(be + 1) * 64], [kT], [fw.reg(("KTS", be))], kT)
            else:
                fw.dma(KT.rearrange("c p t -> p c t")[:, :, g_ * 128:(g_ + 1) * 128], kT.t.rearrange("p (c t) -> p c t", t=128), [kT], [fw.reg("KT")], kT)

        blk(0)
        if g > 0:
            trk(g - 1)
        blk(3)
        blk(1)
        blk(4)
        trq(g)
        blk(2)
        blk(5)
        if smp:
            for be in range(2):
                fw.dma(VSS[be, PAST:PAST + 64, :], vb.t[be * 64:(be + 1) * 64, :], [vb], [fw.reg(("VSS", be))], vb)
        else:
            fw.dma(VS[g * 128:(g + 1) * 128, :], vb.t, [vb], [fw.reg("VS")], vb)
        if g == NG:
            trk(g)

    b2 = [(be, kbk) for be in range(2) for kbk in range(PAST // 128)]

    def ldB2(it):
        be, kbk = b2[it]
        rows = slice(kbk * 128, (kbk + 1) * 128)
        ck = F4[it % 2]; cv = F4[2 + it % 2]
        for i in range(2):
            fw.dma(ck.t[:, i * 512:(i + 1) * 512], c_k[i][be, rows, :], [], [ck], ck)
            fw.dma(cv.t[:, i * 512:(i + 1) * 512], c_v[i][be, rows, :], [], [cv], cv)
    ldB2(0)
    for it in range(len(b2)):
        if it + 1 < len(b2):
            ldB2(it + 1)
        be, kbk = b2[it]
        rows = slice(kbk * 128, (kbk + 1) * 128)
        ck = F4[it % 2]; cv = F4[2 + it % 2]
        kbf = H2[it % 2]; vbf = H2[2 + it % 2]
        fw.op(DVE, [ck], [kbf], "tensor_copy", kbf.t, ck.t)
        fw.op(POOL, [cv], [vbf], "tensor_copy", vbf.t, cv.t)
        fw.dma(VSS[be, rows, :], vbf.t, [vbf], [fw.reg(("VSS", be))], vbf)
        kT = H2[4 + it % 2]
        transpose8(kbf, PS[it % 2], kT)
        fw.dma(KTS[be].rearrange("c p t -> p c t")[:, :, rows], kT.t.rearrange("p (c t) -> p c t", t=128), [kT], [fw.reg(("KTS", be))], kT)

    WdF = Wd.t.rearrange("p a b -> p (a b)")
    WgF = Wg.t.rearrange("p a b -> p (a b)")
    WuF = Wu.t.rearrange("p a b -> p (a b)")
    cur = {}

    def load_kv(Kb, Ksb, Vb, Vsb_, kt_src, kt_reg, v_src, v_reg, nk, vcol):
        n = nk * 128
        for c0 in range(0, n, 2048):
            c1 = min(n, c0 + 2048)
            fw.dma(Ksb[:, c0:c1], kt_src[:, c0:c1], [kt_reg], [Kb], Kb, chain=True, q=POOL)
        vv = v_src.rearrange("(n p) c -> p n c", p=128)
        vs = Vsb_[:, 0:nk * 128].rearrange("p (n c) -> p n c", c=128)
        for k0 in range(0, nk, 8):
            k1 = min(nk, k0 + 8)
            fw.dma(vs[:, k0:k1, :], vv[:, k0:k1, vcol:vcol + 128], [v_reg], [Vb], Vb, chain=True, q=POOL)

    def kvset(kind, be=0):
        if kind == "p":
            return Wg, WgF[:, 0:T], Wu, WuF[:, 0:T]
        return Wd, WdF[:, be * TKS:(be + 1) * TKS], Wd, WdF[:, (2 + be) * TKS:(3 + be) * TKS]

    def ps2(b0):
        return PSALL[:, b0 * 512:(b0 + 2) * 512].rearrange("p (h n) -> p h n", h=2)

    def v2(buf):
        return buf.t.rearrange("p (h n) -> p h n", h=2)

    def load_q(qb, qz_ap, qsrc, qreg, N):
        fw.dma(qz_ap[0:64, 0:N], qsrc[0:64, :], [qreg], [qb], qb)
        fw.dma(qz_ap[64:128, 512:512 + N], qsrc[64:128, :], [qreg], [qb], qb, chain=True)

    QB = []
    for k in range(2):
        v = F4[2 + k].t.bitcast(BF16)
        QB.append((F4[2 + k], v[:, 0:1024], v[:, 1024:2048]))

    def load_q_sb(k, qsrc, qreg, N):
        qb, qz_ap, nq_ap = QB[k]
        load_q(qb, qz_ap, qsrc, qreg, N)
        fw.op(DVE, [qb], [qb], "tensor_scalar", nq_ap, qz_ap, -1.0, None, op0=ALU.mult)

    def sb_slot(qk, N, units, odst, oreg):
        Wg, KTsb, Wu, Vsb = cur["kv"]
        qz, qz_ap, nq_ap = QB[qk]
        nq = qz
        R = H2[4]
        ob = H2[7]
        nu = len(units)
        accs = (PS[6], PS[7])
        tpb = (PS[4], PS[5]); tp2 = ps2(4)
        fw.op(DVE, [], [R], "memset", R.t, 0.0)

        def bufs(ui):
            p = ui % 2
            return (PS[2 * p], PS[2 * p + 1]), ps2(2 * p), F4[p], H2[p], H2[2 + p]

        def stA(ui):
            kb, kn, mbuf, map_, bias = units[ui]
            zb_, z2, e, sp, w = bufs(ui)
            for hh in range(2):
                fw.op(PE, [Wg, qz], [zb_[hh]], "matmul", z2[0:kn, hh, 0:N], KTsb[:, kb * 128:kb * 128 + kn], qz_ap[:, hh * 512:hh * 512 + N], start=True, stop=True)

        def stB(ui):
            kb, kn, mbuf, map_, bias = units[ui]
            zb_, z2, e, sp, w = bufs(ui)
            bb, bap = bias
            fw.op(ACT, [zb_[0], zb_[1], bb], [e], "activation", v2(e)[0:kn, :, 0:N], z2[0:kn, :, 0:N], AF.Exp, bias=bap[0:kn, :], scale=1.0)

        def stB2(ui):
            kb, kn, mbuf, map_, bias = units[ui]
            zb_, z2, e, sp, w = bufs(ui)
            fw.op(ACT, [e], [sp], "activation", v2(sp)[0:kn, :, 0:N], v2(e)[0:kn, :, 0:N], AF.Ln, bias=1.0, scale=1.0)
            if mbuf is not None:
                for hh in range(2):
                    fw.op(DVE, [sp, mbuf], [sp], "tensor_tensor", sp.t[0:kn, hh * 512:hh * 512 + N], sp.t[0:kn, hh * 512:hh * 512 + N], map_, op=ALU.mult)

        def stC(ui):
            kb, kn, mbuf, map_, bias = units[ui]
            zb_, z2, e, sp, w = bufs(ui)
            first = ui == 0
            ks = slice(kb * 128, kb * 128 + kn)
            for hh in range(2):
                cs = slice(hh * 512, hh * 512 + N)
                fw.op(PE, [Wg, nq], [tpb[hh]], "matmul", tp2[0:kn, hh, 0:N], KTsb[:, ks], nq_ap[:, cs], start=True, stop=False, signal=False)
                fw.op(PE, [uinc, sp], [tpb[hh]], "matmul", tp2[0:kn, hh, 0:N], uinc.t[0:kn, 0:kn], sp.t[0:kn, cs], start=False, stop=first, signal=first)
                if not first:
                    fw.op(PE, [ones, R], [tpb[hh]], "matmul", tp2[0:kn, hh, 0:N], ones.t[:, 0:kn], R.t[:, cs], start=False, stop=True)
            if ui != nu - 1:
                fw.op(DVE, [R, sp], [R], "tensor_tensor", v2(R)[0:kn, :, 0:N], v2(R)[0:kn, :, 0:N], v2(sp)[0:kn, :, 0:N], op=ALU.add)

        def stD(ui):
            kb, kn, mbuf, map_, bias = units[ui]
            zb_, z2, e, sp, w = bufs(ui)
            bb, bap = bias
            fw.op(ACT, [tpb[0], tpb[1], bb], [w], "activation", v2(w)[0:kn, :, 0:N], tp2[0:kn, :, 0:N], AF.Exp, bias=bap[0:kn, :], scale=-1.0)
            if mbuf is not None:
                for hh in range(2):
                    fw.op(DVE, [w, mbuf], [w], "tensor_tensor", w.t[0:kn, hh * 512:hh * 512 + N], w.t[0:kn, hh * 512:hh * 512 + N], map_, op=ALU.mult)

        def stE(ui):
            kb, kn, mbuf, map_, bias = units[ui]
            zb_, z2, e, sp, w = bufs(ui)
            for hh in range(2):
                fw.op(PE, [Wu, w], [accs[hh]], "matmul", accs[hh].t[:, 0:N], Vsb[0:kn, kb * 128:(kb + 1) * 128], w.t[0:kn, hh * 512:hh * 512 + N], start=(ui == 0), stop=(ui == nu - 1), signal=(ui == nu - 1))

        for s_ in range(-2, nu):
            if 0 <= s_ + 2 < nu:
                stA(s_ + 2)
            if 0 <= s_ + 1 < nu:
                stB(s_ + 1)
                stB2(s_ + 1)
            if 0 <= s_ < nu:
                stD(s_)
            if 0 <= s_ + 1 < nu:
                stC(s_ + 1)
            if 0 <= s_ < nu:
                stE(s_)
        for hh in range(2):
            pr = slice(64 * hh, 64 * hh + 64)
            fw.op(DVE, [accs[hh]], [ob], "tensor_copy", ob.t[pr, 0:N], accs[hh].t[pr, 0:N])
        fw.dma(odst, ob.t[:, 0:N].rearrange("p (g t) -> p g t", t=128) if N == 512 else ob.t[:, 0:N], [ob], [oreg], ob)

    def df_slot(qk, N, units, odst, oreg):
        Wg, KTsb, Wu, Vsb = cur["kv"]
        qz = H2[5 + qk]
        ob = H2[7]
        accs = (PS[4], PS[5])
        psacc = F4[4]
        nu = len(units)
        fw.op(DVE, [], [psacc], "memset", psacc.t, 0.0)

        def bufs(ui):
            p = ui % 3
            b0 = (0, 2, 6)[p]
            return (PS[b0], PS[b0 + 1]), ps2(b0), H2[p]

        def stA(ui):
            kb, kn, mbuf, map_, bias = units[ui]
            sb_, s2, p = bufs(ui)
            for m in range(2):
                fw.op(PE, [Wg, qz], [sb_[m]], "matmul", s2[0:kn, m, 0:N], KTsb[:, kb * 128:kb * 128 + kn], qz.t[:, m * 512:m * 512 + N], start=True, stop=True)

        def stB(ui):
            kb, kn, mbuf, map_, bias = units[ui]
            sb_, s2, p = bufs(ui)
            bb, bap = bias
            fw.op(ACT, [sb_[0], sb_[1], bb], [p], "activation", v2(p)[0:kn, :, 0:N], s2[0:kn, :, 0:N], AF.Exp, bias=bap[0:kn, :], scale=1.0)
            if mbuf is not None:
                for m in range(2):
                    fw.op(DVE, [p, mbuf], [p], "tensor_tensor", p.t[0:kn, m * 512:m * 512 + N], p.t[0:kn, m * 512:m * 512 + N], map_, op=ALU.mult)
            fw.op(DVE, [psacc, p], [psacc], "tensor_tensor", v2(psacc)[0:kn, :, 0:N], v2(psacc)[0:kn, :, 0:N], v2(p)[0:kn, :, 0:N], op=ALU.add)

        def stC(ui):
            kb, kn, mbuf, map_, bias = units[ui]
            sb_, s2, p = bufs(ui)
            first = ui == 0; last = ui == nu - 1
            for m in range(2):
                fw.op(PE, [Wu, p], [accs[m]], "matmul", accs[m].t[:, 0:N], Vsb[0:kn, kb * 128:(kb + 1) * 128], p.t[0:kn, m * 512:m * 512 + N], start=first, stop=last, signal=last)

        for s_ in range(-2, nu):
            if 0 <= s_ + 2 < nu:
                stA(s_ + 2)
            if 0 <= s_ < nu:
                stB(s_)
                stC(s_)
        sums = (PS[6], PS[7])
        for m in range(2):
            fw.op(PE, [ones32, psacc], [sums[m]], "matmul", sums[m].t[:, 0:N], ones32.t, psacc.t[:, m * 512:m * 512 + N], start=True, stop=True)
        r0 = F4[0]; r1 = F4[1]; a0 = F4[2]; o = F4[3]
        for sm_, r_ in ((sums[0], r0), (sums[1], r1)):
            fw.op(ACT, [sm_], [r_], "activation", r_.t[:, 0:N], sm_.t[:, 0:N], AF.Ln)
            fw.op(ACT, [r_], [r_], "activation", r_.t[:, 0:N], r_.t[:, 0:N], AF.Exp, scale=-1.0)
        fw.op(DVE, [accs[0], r0], [a0], "tensor_tensor", a0.t[:, 0:N], accs[0].t[:, 0:N], r0.t[:, 0:N], op=ALU.mult)
        fw.op(DVE, [accs[1], r1], [r1], "tensor_tensor", r1.t[:, 0:N], accs[1].t[:, 0:N], r1.t[:, 0:N], op=ALU.mult)
        fw.op(DVE, [r1, nlam, a0], [o], "scalar_tensor_tensor", o.t[:, 0:N], r1.t[:, 0:N], nlam.t, a0.t[:, 0:N], op0=ALU.mult, op1=ALU.add)
        sq = H2[3]
        fw.op(DVE, [o], [sq], "tensor_tensor", sq.t[:, 0:N], o.t[:, 0:N], o.t[:, 0:N], op=ALU.mult)
        ms = PS[0]
        fw.op(PE, [ones, sq], [ms], "matmul", ms.t[:, 0:N], ones.t, sq.t[:, 0:N], start=True, stop=True)
        fw.op(ACT, [ms, epsc], [r0], "activation", r0.t[:, 0:N], ms.t[:, 0:N], AF.Ln, bias=epsc.t, scale=1.0 / 128.0)
        fw.op(ACT, [r0], [r0], "activation", r0.t[:, 0:N], r0.t[:, 0:N], AF.Exp, scale=-0.5)
        fw.op(DVE, [o, gcol, r0], [ob], "scalar_tensor_tensor", ob.t[:, 0:N], o.t[:, 0:N], gcol.t, r0.t[:, 0:N], op0=ALU.mult, op1=ALU.mult)
        fw.dma(odst, ob.t[:, 0:N].rearrange("p (g t) -> p g t", t=128) if N == 512 else ob.t[:, 0:N], [ob], [oreg], ob)

    zb = (zcol, zcol.t)
    k0b = (kb0t, kb0t.t)

    def prompt_units(j, desc):
        nkb = 8 * (j + 1)
        order = range(nkb - 1, -1, -1) if desc else range(nkb)
        us = []
        for kb in order:
            r = kb - (8 * j + 4)
            if r >= 0:
                us.append((kb, 128, MSK, MSKv[:, r, :], zb))
            elif kb < 4:
                us.append((kb, 128, None, None, k0b))
            else:
                us.append((kb, 128, None, None, zb))
        return us

    def smp_units(desc, masked):
        us = []
        order = range(NKS - 1, -1, -1) if desc else range(NKS)
        for kb in order:
            if kb == NKS - 1:
                us.append((kb, 64, MSMP if masked else None, MSMP.t[0:64, :] if masked else None, zb))
            else:
                us.append((kb, 128, None, None, zb))
        return us

    fw.op(POOL, [], [H2[5]], "memset", H2[5].t, 0.0)
    fw.op(POOL, [], [H2[6]], "memset", H2[6].t, 0.0)
    fw.op(POOL, [], [F4[2]], "memset", F4[2].t, 0.0)
    fw.op(POOL, [], [F4[3]], "memset", F4[3].t, 0.0)
    segs = []
    for ch in range(8):
        segs.append((ch, "p"))
        segs.append((ch, "s"))

    def seg_load(i):
        ch, kind = segs[i]
        vcol = ch * 128
        if kind == "p":
            Kb, Ka, Vb, Va = kvset("p")
            load_kv(Kb, Ka, Vb, Va, KT[ch], fw.reg("KT"), VS, fw.reg("VS"), T // 128, vcol)
        else:
            for be in range(2):
                Kb, Ka, Vb, Va = kvset("s", be)
                load_kv(Kb, Ka, Vb, Va, KTS[be, ch], fw.reg(("KTS", be)), VSS[be], fw.reg(("VSS", be)), NKS, vcol)

    sbs = []
    for ch in range(4):
        for j in range(8):
            q0 = (2 * j + 1) * 512
            sbs.append((QT[ch, :, q0:q0 + 512], 512))
        for be in range(2):
            sbs.append((QT[ch, :, T + be * 64:T + be * 64 + 64], 64))
    sbi = {"i": 0}

    dfs = []
    for ch in range(4, 8):
        for j in range(8):
            q0 = (2 * j + 1) * 512
            dfs.append((QT[ch, :, q0:q0 + 512], 512))
        for be in range(2):
            dfs.append((QT[ch, :, T + be * 64:T + be * 64 + 64], 64))
    dfi = {"i": 0}

    def run_df(units, N, odst):
        i = dfi["i"]
        if i == 0:
            load_q(H2[5], H2[5].t, dfs[0][0], fw.reg("QT"), dfs[0][1])
        if i + 1 < len(dfs):
            qb = H2[5 + (i + 1) % 2]
            load_q(qb, qb.t, dfs[i + 1][0], fw.reg("QT"), dfs[i + 1][1])
        df_slot(i % 2, N, units, odst, fw.reg("OT"))
        dfi["i"] = i + 1

    def run_sb(units, N, odst):
        i = sbi["i"]
        if i + 1 < len(sbs):
            load_q_sb((i + 1) % 2, sbs[i + 1][0], fw.reg("QT"), sbs[i + 1][1])
        sb_slot(i % 2, N, units, odst, fw.reg("OT"))
        sbi["i"] = i + 1

    def seg_run(i):
        ch, kind = segs[i]
        issb = ch < 4
        if kind == "p":
            cur["kv"] = kvset("p")
            if ch in (0, 4):
                fw.dma(MSKv, msbd if issb else mdfd, [], [MSK], MSK)
            for j in range(8):
                q0 = (2 * j + 1) * 512
                od = OT[4 * j:4 * j + 4, :, ch, :].rearrange("g p t -> p g t")
                if issb:
                    run_sb(prompt_units(j, True), 512, od)
                else:
                    run_df(prompt_units(j, False), 512, od)
        else:
            for be in range(2):
                cur["kv"] = kvset("s", be)
                qs = QT[ch, :, T + be * 64:T + be * 64 + 64]
                od = OT[NG // 2, :, ch, be * 64:(be + 1) * 64]
                if issb:
                    run_sb(smp_units(True, True), 64, od)
                else:
                    run_df(smp_units(False, False), 64, od)

    load_q_sb(0, sbs[0][0], fw.reg("QT"), sbs[0][1])
    seg_load(0)
    for i in range(len(segs)):
        if i + 1 < len(segs):
            seg_load(i + 1)
        seg_run(i)

    for c in range(8):
        for q in range(2):
            load_cast(Wd, Wd.t[:, c, q * 512:(q + 1) * 512], wo[c * 128:(c + 1) * 128, q * 512:(q + 1) * 512], 512)
    load_ln(2)
    NGD = NG // 2 + 1

    def x1rows(g):
        if g < NG // 2:
            s = 2 * (g // 4) + 1
            r0 = s * 512 + (g % 4) * 128
            return X1[r0:r0 + 128, :], fw.reg(("X1", r0 // 128))
        return X1[T:T + 128, :], fw.reg(("X1", NG))

    LNS = []
    for k in range(3):
        LNS.append((sb("stats%d" % k, [128, 2, 6], F32), sb("mv%d" % k, [128, 2], F32), sb("lnv%d" % k, [128, 1], F32), sb("rstd%d" % k, [128, 1], F32)))

    def ldD(g):
        og = H2[g % 3]; rt = F4[g % 4]
        fw.dma(og.t, OT[g].rearrange("p c t -> p (c t)"), [fw.reg("OT")], [og], og)
        ap, rb = x1rows(g)
        fw.dma(rt.t, ap, [rb], [rt], rt)

    def D1a(g):
        og = H2[g % 3]; rt = F4[g % 4]
        st_, mv_, lnv_, rstd_ = LNS[g % 3]
        for half in range(2):
            for c in range(8):
                fw.op(PE, [og, Wd], [PS[half]], "matmul", PS[half].t, og.t[:, c * 128:(c + 1) * 128], Wd.t[:, c, half * 512:(half + 1) * 512], start=(c == 0), stop=(c == 7), signal=(c == 7))
        for half in range(2):
            fw.op(DVE, [PS[half], rt], [rt], "scalar_tensor_tensor", rt.t[:, half * 512:(half + 1) * 512], PS[half].t, 1.0 / ALPHA, rt.t[:, half * 512:(half + 1) * 512], op0=ALU.mult, op1=ALU.add)
        for h in range(2):
            fw.op(DVE, [rt], [st_], "bn_stats", st_.t[:, h, :], rt.t[:, h * 512:(h + 1) * 512])
        fw.op(DVE, [st_], [mv_], "bn_aggr", mv_.t, st_.t)

    def D1b(g):
        st_, mv_, lnv_, rstd_ = LNS[g % 3]
        fw.op(ACT, [mv_, epsl], [lnv_], "activation", lnv_.t, mv_.t[:, 1:2], AF.Ln, bias=epsl.t, scale=1.0)
        fw.op(ACT, [lnv_], [rstd_], "activation", rstd_.t, lnv_.t, AF.Exp, scale=-0.5)

    def D1c(g):
        rt = F4[g % 4]
        st_, mv_, lnv_, rstd_ = LNS[g % 3]
        fw.op(DVE, [rt, mv_, rstd_], [rt], "tensor_scalar", rt.t, rt.t, mv_.t[:, 0:1], rstd_.t, op0=ALU.subtract, op1=ALU.mult)
        fw.op(POOL, [rt, F4[4]], [rt], "tensor_tensor", rt.t, rt.t, F4[4].t, op=ALU.mult)
        fw.op(POOL, [rt, F4[5]], [rt], "tensor_tensor", rt.t, rt.t, F4[5].t, op=ALU.add)
        fw.dma(X2[g * 128:(g + 1) * 128, :], rt.t, [rt], [fw.reg(("X2", g))], rt)

    ldD(0)
    ldD(1)
    f2chunks = ffn_w_chunks(f2)
    f2i = {"i": 0}
    for g in range(NGD + 2):
        if g < NGD:
            D1a(g)
            for _ in range(2):
                if f2i["i"] < 64:
                    load_cast(*f2chunks[f2i["i"]])
                    f2i["i"] += 1
        if 0 <= g - 1 < NGD:
            D1b(g - 1)
        if 0 <= g - 2 < NGD:
            D1c(g - 2)
        if g + 2 < NGD:
            ldD(g + 2)

    def srcD(g):
        return X2[g * 128:(g + 1) * 128, :], [fw.reg(("X2", g))]

    def dstD(g, rt):
        if g < NG // 2:
            fw.dma(yp[g * 128:(g + 1) * 128, :], rt.t, [rt], [], rt, is_output=True)
        else:
            fw.dma(ys, rt.t, [rt], [], rt, is_output=True)

    ffn_stage(NGD, srcD, f2, 4, dstD, False, skip=f2i["i"])
    fw.finish()
    return nc


def _consts():
    bf = ml_dtypes.bfloat16
    p = np.arange(128)[:, None]
    ident = (p == np.arange(128)[None, :]).astype(np.float32).astype(bf)
    uinc = (p >= np.arange(128)[None, :]).astype(np.float32).astype(bf)
    t = np.arange(512)[None, None, :]
    r = np.arange(4)[None, :, None]
    s = 128 * r + p[:, :, None]
    msb = (s < t).astype(np.float32).astype(bf)
    mdf = ((s // 64) <= (t // 64)).astype(np.float32).astype(bf)
    msmp = (p < np.arange(64)[None, :]).astype(np.float32).astype(bf)
    return ident, uinc, np.ascontiguousarray(msb), np.ascontiguousarray(mdf), msmp


def _rope_tab(pos):
    half = 32
    inv = (10000.0 ** (-np.arange(half, dtype=np.float32) / half)).astype(np.float32)
    ang = pos.astype(np.float32)[:, None] * inv[None, :]
    return np.cos(ang).astype(np.float32), np.sin(ang).astype(np.float32)


def kernel(**inp):
    f = lambda k: np.ascontiguousarray(np.asarray(inp[k], dtype=np.float32))
    x_prompt = f("x_prompt"); x_sample = f("x_sample")
    ident, uinc, msb, mdf, msmp = _consts()
    nc = build()
    coss, sins = _rope_tab(PAST + np.concatenate([np.arange(64), np.arange(64)]))
    shared = {
        "f1g": f("ffn1_wg")[0], "f1u": f("ffn1_wu")[0], "f1d": f("ffn1_wd")[0],
        "f2g": f("ffn2_wg")[0], "f2u": f("ffn2_wu")[0], "f2d": f("ffn2_wd")[0],
        "win": f("w_in")[0], "wo": f("w_o")[0],
        "ln1g": f("ln1_g"), "ln1b": f("ln1_b"), "ln2g": f("ln2_g"), "ln2b": f("ln2_b"), "ln3g": f("ln3_g"), "ln3b": f("ln3_b"),
        "lq1": f("lambda_q1"), "lk1": f("lambda_k1"), "lq2": f("lambda_q2"), "lk2": f("lambda_k2"),
        "subg": np.ascontiguousarray(f("subln_g").reshape(128, 1)),
        "coss": coss, "sins": sins, "ident": ident, "uinc": uinc, "msb": msb, "mdf": mdf, "msmp": msmp,
    }
    csk = f("cache_sb_k")[0].reshape(16, PAST, 512); csv = f("cache_sb_v")[0].reshape(16, PAST, 512)
    cdk = f("cache_diff_k")[0].reshape(16, PAST, 512); cdv = f("cache_diff_v")[0].reshape(16, PAST, 512)
    in_maps = []
    for c in range(8):
        b, h = c // 2, c % 2
        if h == 0:
            xpc = np.concatenate([np.zeros((512, D), np.float32), x_prompt[b, :T - 512]], axis=0)
            pos = np.arange(T) - 512
        else:
            xpc = x_prompt[b]
            pos = np.arange(T)
        cp, sp = _rope_tab(np.maximum(pos, 0))
        m = dict(shared)
        m.update({
            "xp": np.ascontiguousarray(xpc), "xs": np.ascontiguousarray(x_sample[2 * c:2 * c + 2].reshape(128, D)),
            "c_sbk": np.ascontiguousarray(csk[2 * c:2 * c + 2]), "c_sbv": np.ascontiguousarray(csv[2 * c:2 * c + 2]),
            "c_dfk": np.ascontiguousarray(cdk[2 * c:2 * c + 2]), "c_dfv": np.ascontiguousarray(cdv[2 * c:2 * c + 2]),
            "cosp": cp, "sinp": sp,
            "kb0": np.full((128, 1), NEGBIG if h == 0 else 0.0, np.float32),
        })
        in_maps.append(m)
    res = run_bass_kernel_spmd(nc, in_maps, core_ids=list(range(8)))
    R = res.results
    y_p = np.zeros((4, T, D), np.float32)
    y_s = np.zeros((16, 64, D), np.float32)
    kvp = [np.zeros((1, 4, T, 512), np.float32) for _ in range(4)]
    kvs = [np.zeros((1, 16, 64, 512), np.float32) for _ in range(4)]
    names_p = ("o_sbk", "o_sbv", "o_dfk", "o_dfv"); names_s = ("s_sbk", "s_sbv", "s_dfk", "s_dfv")
    for c in range(8):
        b, h = c // 2, c % 2
        yo = R[c]["yp"].reshape(8, 512, D)
        for j in range(8):
            sblk = 2 * j + h
            y_p[b, sblk * 512:(sblk + 1) * 512] = yo[j]
        y_s[2 * c:2 * c + 2] = R[c]["ys"].reshape(2, 64, D)
        for i in range(4):
            kvs[i][0, 2 * c:2 * c + 2] = R[c][names_s[i]].reshape(2, 64, 512)
            if h == 1:
                kvp[i][0, b] = R[c][names_p[i]]
    return (y_p, y_s,
            kvp[0].reshape(1, 4, T, 8, 64), kvp[1].reshape(1, 4, T, 8, 64),
            kvp[2].reshape(1, 4, T, 4, 2, 64), kvp[3].reshape(1, 4, T, 4, 128),
            kvs[0].reshape(1, 16, 64, 8, 64), kvs[1].reshape(1, 16, 64, 8, 64),
            kvs[2].reshape(1, 16, 64, 4, 2, 64), kvs[3].reshape(1, 16, 64, 4, 128))
```
